# Optimizing a Trainium2 kernel written in Bass

```python
import jax, jax.numpy as jnp
from jax import lax
import numpy as np

D_MODEL = 2048
BATCH = 1
SEQ = 16384
DEPTH = 2
DEC_BATCH = 32
DEC_SEQ = 16
PAST_LEN = 2048

CHUNK = 64
N_MIXERS = 2
N_POOL_LAYERS = (DEPTH + 1) // 2
N_FOX_LAYERS = DEPTH // 2
POOL_WINDOWS = (2, 4, 8, 16)
N_POOL_GROUPS = len(POOL_WINDOWS)
POOL_GROUP = D_MODEL // N_POOL_GROUPS
POOL_STATE = max(POOL_WINDOWS) - 1
FOX_HEAD_DIM = 128
FOX_HEADS = D_MODEL // FOX_HEAD_DIM
D_FF = 4 * D_MODEL
Q_BLOCK = 128
LN_EPS = 1e-5
DN_ALPHA = (2 * DEPTH) ** 0.25
DN_BETA = (8 * DEPTH) ** -0.25
FORGET_BIAS_INIT = 3.0

kernel_name = "pool_fox_macaron_deepnorm_stream_step"


def layer_norm(x, g, b):
    xf = x.astype(jnp.float32)
    mu = jnp.mean(xf, axis=-1, keepdims=True)
    var = jnp.mean(jnp.square(xf - mu), axis=-1, keepdims=True)
    return ((xf - mu) * lax.rsqrt(var + LN_EPS) * g + b).astype(x.dtype)


def swiglu(x, w1, w3, w2):
    return (jax.nn.silu(x @ w1) * (x @ w3)) @ w2


def pool_mix(x, prev, start_pos, w_grp, scale):
    B, T, D = x.shape
    xr = jnp.concatenate([prev, x], axis=1)
    xe = xr.astype(jnp.float32)
    cs = jnp.concatenate([jnp.zeros_like(xe[:, :1]), jnp.cumsum(xe, axis=1)], axis=1)
    end = cs[:, POOL_STATE + 1:]
    pos = start_pos + jnp.arange(T)
    diffs = []
    for g, w in enumerate(POOL_WINDOWS):
        sl = slice(g * POOL_GROUP, (g + 1) * POOL_GROUP)
        win = end[:, :, sl] - cs[:, POOL_STATE + 1 - w:POOL_STATE + 1 - w + T, sl]
        cnt = jnp.minimum(pos + 1, w).astype(jnp.float32)[None, :, None]
        diffs.append(win / cnt - xe[:, POOL_STATE:, sl])
    d = jnp.stack(diffs, axis=2).astype(x.dtype)
    y = jnp.einsum('btgc,gcd->btgd', d, w_grp).reshape(B, T, D)
    return y * scale, xr[:, -POOL_STATE:]


def fox_project(x, w_in, b_f):
    B, T, D = x.shape
    p = x @ w_in
    q, k, v, fl = jnp.split(p, [D, 2 * D, 3 * D], axis=-1)
    shp = (B, T, FOX_HEADS, FOX_HEAD_DIM)
    logf = jax.nn.log_sigmoid(fl.astype(jnp.float32) + b_f.astype(jnp.float32))
    return q.reshape(shp), k.reshape(shp), v.reshape(shp), logf


def fox_attend_prompt(q, k, v, logf):
    B, S, H, DH = q.shape
    scale = DH ** -0.5
    c = jnp.cumsum(logf, axis=1)
    cT = c.transpose(0, 2, 1)
    nb = S // Q_BLOCK
    qb = q.reshape(B, nb, Q_BLOCK, H, DH).transpose(1, 0, 2, 3, 4)
    cb = cT.reshape(B, H, nb, Q_BLOCK).transpose(2, 0, 1, 3)
    kpos = jnp.arange(S)

    def block(args):
        i, qi, ci = args
        s = jnp.einsum('bqhd,bkhd->bhqk', qi, k, preferred_element_type=jnp.float32) * scale
        s = s + ci[..., None] - cT[:, :, None, :]
        qpos = i * Q_BLOCK + jnp.arange(Q_BLOCK)
        s = jnp.where(kpos[None, :] <= qpos[:, None], s, -jnp.inf)
        p = jax.nn.softmax(s, axis=-1)
        return jnp.einsum('bhqk,bkhd->bqhd', p.astype(v.dtype), v)

    o = lax.map(block, (jnp.arange(nb), qb, cb))
    return o.transpose(1, 0, 2, 3, 4).reshape(B, S, H * DH)


def fox_attend_sample(q, k_new, v_new, logf_new, k_cache, v_cache, logf_cache):
    B, T, H, DH = q.shape
    P = k_cache.shape[1]
    scale = DH ** -0.5
    k = jnp.concatenate([k_cache.astype(k_new.dtype), k_new], axis=1)
    v = jnp.concatenate([v_cache.astype(v_new.dtype), v_new], axis=1)
    lf = jnp.concatenate([logf_cache.astype(jnp.float32), logf_new], axis=1)
    cT = jnp.cumsum(lf, axis=1).transpose(0, 2, 1)
    s = jnp.einsum('bqhd,bkhd->bhqk', q, k, preferred_element_type=jnp.float32) * scale
    s = s + cT[:, :, P:, None] - cT[:, :, None, :]
    qpos = P + jnp.arange(T)
    kpos = jnp.arange(P + T)
    s = jnp.where(kpos[None, :] <= qpos[:, None], s, -jnp.inf)
    p = jax.nn.softmax(s, axis=-1)
    o = jnp.einsum('bhqk,bkhd->bqhd', p.astype(v.dtype), v)
    return o.reshape(B, T, H * DH)


def run_trunk(x, start_pos, pool_prev, fox_cache, ln_g, ln_b, ffn_w1, ffn_w3, ffn_w2,
              pool_w, pool_scale, fox_w_in, fox_b_f, fox_w_o):
    B = x.shape[0]
    pool_new, k_new, v_new, lf_new = [], [], [], []
    for i in range(DEPTH):
        j = i // N_MIXERS
        h = swiglu(x, ffn_w1[i, 0], ffn_w3[i, 0], ffn_w2[i, 0])
        x = layer_norm(DN_ALPHA * x + 0.5 * h, ln_g[i, 0], ln_b[i, 0])
        if i % N_MIXERS == 0:
            prev = jnp.zeros((B, POOL_STATE, D_MODEL), x.dtype) if pool_prev is None else pool_prev[j].astype(x.dtype)
            m, st = pool_mix(x, prev, start_pos, pool_w[j], pool_scale[j])
            pool_new.append(st)
        else:
            q, k, v, lf = fox_project(x, fox_w_in[j], fox_b_f[j])
            if fox_cache is None:
                o = fox_attend_prompt(q, k, v, lf)
            else:
                ck, cv, cl = fox_cache
                o = fox_attend_sample(q, k, v, lf, ck[j], cv[j], cl[j])
            m = o @ fox_w_o[j]
            k_new.append(k)
            v_new.append(v)
            lf_new.append(lf)
        x = layer_norm(DN_ALPHA * x + m, ln_g[i, 1], ln_b[i, 1])
        h = swiglu(x, ffn_w1[i, 1], ffn_w3[i, 1], ffn_w2[i, 1])
        x = layer_norm(DN_ALPHA * x + 0.5 * h, ln_g[i, 2], ln_b[i, 2])
    return x, jnp.stack(pool_new), jnp.stack(k_new), jnp.stack(v_new), jnp.stack(lf_new)


def setup_inputs(seed: int = 0) -> dict:
    key = jax.random.key(seed)
    ks = jax.random.split(key, 20)
    f32 = jnp.float32

    def nrm(k, shape, s):
        return jax.random.normal(k, shape, f32) * s

    D, H = D_MODEL, FOX_HEADS
    x_prompt = nrm(ks[0], (BATCH, SEQ, D), 1.0)
    x_sample = nrm(ks[1], (DEC_BATCH, DEC_SEQ, D), 1.0)
    state_pool = nrm(ks[2], (N_POOL_LAYERS, DEC_BATCH, POOL_STATE, D), 1.0)
    cache_fox_k = nrm(ks[3], (N_FOX_LAYERS, DEC_BATCH, PAST_LEN, H, FOX_HEAD_DIM), 1.0)
    cache_fox_v = nrm(ks[4], (N_FOX_LAYERS, DEC_BATCH, PAST_LEN, H, FOX_HEAD_DIM), 1.0)
    cache_fox_logf = jax.nn.log_sigmoid(FORGET_BIAS_INIT + nrm(ks[5], (N_FOX_LAYERS, DEC_BATCH, PAST_LEN, H), 1.0))
    ln_g = 1.0 + nrm(ks[6], (DEPTH, 3, D), 0.02)
    ln_b = nrm(ks[7], (DEPTH, 3, D), 0.02)
    ffn_w1 = nrm(ks[8], (DEPTH, 2, D, D_FF), D ** -0.5)
    ffn_w3 = nrm(ks[9], (DEPTH, 2, D, D_FF), D ** -0.5)
    ffn_w2 = nrm(ks[10], (DEPTH, 2, D_FF, D), D_FF ** -0.5 * DN_BETA)
    pool_w = nrm(ks[11], (N_POOL_LAYERS, N_POOL_GROUPS, POOL_GROUP, POOL_GROUP), POOL_GROUP ** -0.5 * DN_BETA)
    pool_scale = 1.0 + nrm(ks[12], (N_POOL_LAYERS, D), 0.02)
    w_qk = nrm(ks[13], (N_FOX_LAYERS, D, 2 * D), D ** -0.5)
    w_v = nrm(ks[14], (N_FOX_LAYERS, D, D), D ** -0.5 * DN_BETA)
    w_f = nrm(ks[15], (N_FOX_LAYERS, D, H), D ** -0.5)
    fox_w_in = jnp.concatenate([w_qk, w_v, w_f], axis=-1)
    fox_b_f = FORGET_BIAS_INIT + nrm(ks[16], (N_FOX_LAYERS, H), 0.1)
    fox_w_o = nrm(ks[17], (N_FOX_LAYERS, D, D), D ** -0.5 * DN_BETA)
    return {"x_prompt": x_prompt, "x_sample": x_sample, "state_pool": state_pool,
            "cache_fox_k": cache_fox_k, "cache_fox_v": cache_fox_v, "cache_fox_logf": cache_fox_logf,
            "ln_g": ln_g, "ln_b": ln_b, "ffn_w1": ffn_w1, "ffn_w3": ffn_w3, "ffn_w2": ffn_w2,
            "pool_w": pool_w, "pool_scale": pool_scale, "fox_w_in": fox_w_in, "fox_b_f": fox_b_f,
            "fox_w_o": fox_w_o}


def reference(x_prompt, x_sample, state_pool, cache_fox_k, cache_fox_v, cache_fox_logf,
              ln_g, ln_b, ffn_w1, ffn_w3, ffn_w2, pool_w, pool_scale, fox_w_in, fox_b_f, fox_w_o):
    y_prompt, pool_p, k_p, v_p, lf_p = run_trunk(
        x_prompt, 0, None, None, ln_g, ln_b, ffn_w1, ffn_w3, ffn_w2,
        pool_w, pool_scale, fox_w_in, fox_b_f, fox_w_o)
    past = cache_fox_k.shape[2]
    y_sample, pool_s, k_s, v_s, lf_s = run_trunk(
        x_sample, past, state_pool, (cache_fox_k, cache_fox_v, cache_fox_logf),
        ln_g, ln_b, ffn_w1, ffn_w3, ffn_w2, pool_w, pool_scale, fox_w_in, fox_b_f, fox_w_o)
    return (y_prompt, y_sample, pool_p, pool_s, k_p, v_p, lf_p, k_s, v_s, lf_s)
```

```python
import os
from contextlib import ExitStack

import numpy as np
import concourse.bass as bass
import concourse.mybir as mybir
from concourse.bass_utils import run_bass_kernel_spmd

F32 = mybir.dt.float32
BF16 = mybir.dt.bfloat16
ALU = mybir.AluOpType
AF = mybir.ActivationFunctionType

NCORE = 8
D = 2048
DFF = 8192
NCH = 16
H = 16
PCH = 2048
HALO = 16
NSEQ = 4
TSQ = 16
PAST = 2048
ALPHA = float(4 ** 0.25)
EPS = 1e-5
NEG = -30000.0
TP = 1024
NBLK = PCH // 128
SFROWS = PCH + 128
SCALE = float(128 ** -0.5)
WC = 256
WSLOT = 16 * WC
NSLOT = 6
ARENA = 66048
TOKSTG = 57856
MISC = 7168

STAGE = int(os.environ.get("MK_STAGE", "3"))
NPT = int(os.environ.get("MK_NPT", str(PCH // TP)))
STOP = int(os.environ.get("MK_STOP", "99"))
SMALL = bool(int(os.environ.get("MK_SMALL", "0")))
CPAST = 128 if SMALL else PAST
DFFQ = int(os.environ.get("MK_DFFQ", "4"))
DFFX = 2048 * DFFQ
NRUN = int(os.environ.get("MK_CORES", str(NCORE)))


class Op:
    __slots__ = ("eng", "fn", "deps", "sig", "tok", "dsem")

    def __init__(self, eng, fn, deps, dsem=None):
        self.eng = eng
        self.fn = fn
        self.deps = deps
        self.sig = False
        self.tok = None
        self.dsem = dsem


class DSem:
    def __init__(self, handle):
        self.h = handle
        self.count = 0


ENGS = ["tensor", "vector", "scalar", "gpsimd", "sync"]


def flat(deps):
    out = []
    for d in deps:
        if d is None:
            continue
        if isinstance(d, (list, tuple)):
            out.extend(flat(d))
        else:
            out.append(d)
    return out


class Prog:
    def __init__(self, nc, stack):
        self.nc = nc
        self.stack = stack
        self.q = {e: [] for e in ENGS}
        self.sem = {e: stack.enter_context(nc.semaphore("e_" + e)) for e in ENGS}
        self.dry = True
        self.last = {e: None for e in ENGS}
        self.nds = 0

    def dsem(self):
        self.nds += 1
        return DSem(self.stack.enter_context(self.nc.semaphore("d%d" % self.nds)))

    def add(self, eng, fn, deps=()):
        if self.dry:
            return None
        op = Op(eng, fn, flat(deps))
        self.q[eng].append(op)
        self.last[eng] = op
        return op

    def dma(self, eng, out, in_, deps=(), dsem=None):
        if self.dry:
            return None
        assert tuple(out.shape) == tuple(in_.shape) or out.size() == in_.size(), (out.shape, in_.shape)
        op = self.add(eng, lambda e: e.dma_start(out=out, in_=in_), deps)
        op.dsem = dsem
        return op

    def barrier(self, engs=("tensor", "vector", "scalar", "sync")):
        if self.dry:
            return
        lasts = [self.last[e] for e in engs if self.last[e] is not None]
        for e in engs:
            self.add(e, None, [l for l in lasts if l.eng != e])

    def flush(self, block):
        for e in ENGS:
            for op in self.q[e]:
                for d in op.deps:
                    d.sig = True
        for e in ENGS:
            c = 0
            for op in self.q[e]:
                if op.fn is None:
                    continue
                if op.dsem is not None:
                    op.dsem.count += 16
                    op.tok = (op.dsem.h, op.dsem.count)
                elif op.sig:
                    c += 1
                    op.tok = (self.sem[e], c)
            assert c < 60000, (e, c)
        nwait = {e: 0 for e in ENGS}

        def run(e, engine):
            waited = {}
            for op in self.q[e]:
                for d in op.deps:
                    t = d.tok
                    if t is None:
                        continue
                    if d.eng == "tensor" and e == "tensor" and d.dsem is None:
                        continue
                    key = id(t[0])
                    if waited.get(key, 0) >= t[1]:
                        continue
                    engine.wait_ge(t[0], t[1])
                    nwait[e] += 1
                    waited[key] = t[1]
                if op.fn is None:
                    continue
                ins = op.fn(engine)
                if op.tok is not None:
                    ins.then_inc(op.tok[0], 16 if op.dsem is not None else 1)

        @block.tensor
        def _(t):
            run("tensor", t)

        @block.vector
        def _(v):
            run("vector", v)

        @block.scalar
        def _(s):
            run("scalar", s)

        @block.gpsimd
        def _(g):
            run("gpsimd", g)

        @block.sync
        def _(s):
            run("sync", s)

        if os.environ.get("MK_VERBOSE"):
            print("ops:", {e: len(self.q[e]) for e in ENGS}, "waits:", nwait, flush=True)


class Bank:
    def __init__(self, ap):
        self.ap = ap
        self.readers = []


class Rot:
    def __init__(self, items):
        self.items = items
        self.i = 0

    def next(self):
        it = self.items[self.i % len(self.items)]
        self.i += 1
        return it


class Buf:
    def __init__(self, ap, dsem=None):
        self.ap = ap
        self.readers = []
        self.dsem = dsem


class WStream:
    def __init__(self, P, slots):
        self.P = P
        self.slots = slots
        self.plan = []
        self.reset()

    def reset(self):
        self.nissued = 0
        self.nget = 0
        self.nrel = 0
        self.loads = {}
        for s in self.slots:
            s.readers = []

    def _pump(self):
        n = len(self.slots)
        while self.nissued < len(self.plan) and self.nissued < self.nrel + n:
            j = self.nissued
            src, kc, ncol, key = self.plan[j]
            slot = self.slots[j % n]
            view = slot.ap[:, 0:kc * ncol].rearrange("p (k n) -> p k n", k=kc)
            op = self.P.dma("gpsimd", view, src, deps=list(slot.readers) + [self.wdep[key]], dsem=slot.dsem)
            slot.readers = []
            self.loads[j] = (view, op, slot)
            self.nissued += 1

    def get(self, src, kc, ncol, key):
        i = self.nget
        self.nget += 1
        if self.P.dry:
            self.plan.append([src, kc, ncol, key])
            return (None, None, None)
        assert self.plan[i][1] == kc and self.plan[i][2] == ncol and self.plan[i][3] == key
        self._pump()
        return self.loads.pop(i)

    def release(self, slot, op):
        if self.P.dry:
            return
        slot.readers.append(op)
        self.nrel += 1
        self._pump()


class Ctx:
    pass


def build_program():
    nc = bass.Bass("TRN2", target_bir_lowering=False)

    def din(name, shape, dt=F32):
        return nc.dram_tensor(name, list(shape), dt, kind="ExternalInput").ap()

    def dout(name, shape, dt=F32):
        return nc.dram_tensor(name, list(shape), dt, kind="ExternalOutput").ap()

    I = Ctx()
    I.xp = din("xp", [8 * PCH, D])
    I.xh = din("xh", [128, D])
    I.xs = din("xs", [NSEQ * TSQ, D])
    I.sp = din("sp", [NSEQ * 15, D])
    I.ck = din("ck", [NSEQ, CPAST, D])
    I.cv = din("cv", [NSEQ, CPAST, D])
    I.cl = din("cl", [NSEQ, CPAST, H])
    I.w1 = din("w1", [4, D, DFFX])
    I.w3 = din("w3", [4, D, DFFX])
    I.w2 = din("w2", [4, DFFX, D])
    I.pw = din("pw", [D, 512])
    I.win = din("win", [D, 3 * D + H])
    I.wo = din("wo", [D, D])
    I.psc = din("psc", [128, NCH])
    I.bf = din("bf", [128, H])
    I.lng = din("lng", [128, 6 * NCH])
    I.lnb = din("lnb", [128, 6 * NCH])
    I.cst = din("cst", [128, 512])
    I.meta = din("meta", [128, 128])
    I.wsel = din("wsel", [8, 8 * 128])

    O = Ctx()
    O.yp = dout("yp", [PCH, D])
    O.ys = dout("ys", [NSEQ * TSQ, D])
    O.pp = dout("pp", [15, D])
    O.ps = dout("ps", [NSEQ * 15, D])
    O.kp = dout("kp", [PCH, D])
    O.vp = dout("vp", [PCH, D])
    O.lp = dout("lp", [PCH, H])
    O.ks = dout("ks", [NSEQ * TSQ, D])
    O.vs = dout("vs", [NSEQ * TSQ, D])
    O.ls = dout("ls", [NSEQ * TSQ, H])

    S = Ctx()
    S.XR = nc.dram_tensor("XR", [D, PCH], F32).ap()
    S.QT = nc.dram_tensor("QT", [D, PCH], BF16).ap()
    S.OT = nc.dram_tensor("OT", [D, PCH], BF16).ap()
    S.KV = nc.dram_tensor("KV", [2 * D, 8 * PCH], BF16).ap()
    S.SFG = nc.dram_tensor("SFG", [8 * SFROWS, H], F32).ap()
    S.HSC = nc.dram_tensor("HSC", [128, 8 * 256], F32).ap()
    with ExitStack() as st:
        arena = st.enter_context(nc.sbuf_tensor("arena", [128, ARENA], BF16))
        wring = st.enter_context(nc.sbuf_tensor("wring", [128, NSLOT * WSLOT], BF16))
        misc = st.enter_context(nc.sbuf_tensor("misc", [128, MISC], F32))
        psum = [st.enter_context(nc.psum_tensor("ps%d" % i, [128, 512], F32)) for i in range(8)]
        P = Prog(nc, st)
        banks = [Bank(p[:]) for p in psum]
        ws = WStream(P, [Buf(wring[:, i * WSLOT:(i + 1) * WSLOT], P.dsem()) for i in range(NSLOT)])
        E = Emitter(nc, P, ws, I, O, S, arena[:], misc[:], banks)
        P.dry = True
        E.emit()
        ws.reset()
        P.dry = False
        E.emit()
        with nc.Block() as block:
            P.flush(block)
    return nc


class Emitter:
    def __init__(self, nc, P, ws, I, O, S, arena, misc, banks):
        self.nc, self.P, self.ws = nc, P, ws
        self.I, self.O, self.S = I, O, S
        self.arena, self.misc, self.banks = arena, misc, banks
        self.dsems = {}

    def ds(self, key):
        if key not in self.dsems:
            self.dsems[key] = self.P.dsem()
        return self.dsems[key]

    def f32view(self, off_bf, nelem_f32):
        return self.arena[:, off_bf:off_bf + 2 * nelem_f32].bitcast(F32)

    def mm(self, out, lhsT, rhs, start, stop, deps=()):
        return self.P.add("tensor", lambda e: e.matmul(out, lhsT, rhs, start=start, stop=stop), deps)

    def tr(self, out, in_, ident, deps=()):
        return self.P.add("tensor", lambda e: e.transpose(out, in_, ident), deps)

    def act(self, out, in_, func, bias=None, scale=None, deps=()):
        kw = {}
        if bias is not None:
            kw["bias"] = bias
        if scale is not None:
            kw["scale"] = scale
        return self.P.add("scalar", lambda e: e.activation(out, in_, func, **kw), deps)

    def vtt(self, out, a, b, op, deps=()):
        return self.P.add("vector", lambda e: e.tensor_tensor(out, a, b, op), deps)

    def vts(self, out, a, s1, s2, op0, op1=None, deps=()):
        if op1 is None:
            return self.P.add("vector", lambda e: e.tensor_scalar(out, a, s1, None, op0), deps)
        return self.P.add("vector", lambda e: e.tensor_scalar(out, a, s1, s2, op0, op1), deps)

    def vstt(self, out, a, s, b, op0, op1, deps=()):
        return self.P.add("vector", lambda e: e.scalar_tensor_tensor(out, a, s, b, op0, op1), deps)

    def vcopy(self, out, in_, deps=()):
        return self.P.add("vector", lambda e: e.tensor_copy(out, in_), deps)

    def acopy(self, out, in_, deps=(), scale=None):
        if scale is None:
            return self.P.add("scalar", lambda e: e.activation(out, in_, AF.Copy), deps)
        return self.P.add("scalar", lambda e: e.activation(out, in_, AF.Identity, scale=scale), deps)

    def wsrc(self, key, r0, nrows, c0, ncols):
        I = self.I
        if key[0:2] in ("w1", "w3", "w2"):
            src = {"w1": I.w1, "w3": I.w3, "w2": I.w2}[key[0:2]][int(key[3:])]
        else:
            src = {"pw": I.pw, "win": I.win, "wo": I.wo}[key]
        return src[r0:r0 + nrows, c0:c0 + ncols].rearrange("(k p) n -> p k n", p=128)

    def prep_weights(self):
        self.wdep = {}
        self.ws.wdep = self.wdep
        for i in range(4):
            for k in ("w1", "w3", "w2"):
                self.wdep["%s_%d" % (k, i)] = None
        for k in ("pw", "win", "wo"):
            self.wdep[k] = None

    def setup(self):
        P, m = self.P, self.misc
        C = Ctx()
        self.C = C
        o = 0

        def take(n):
            nonlocal o
            v = m[:, o:o + n]
            o += n
            return v

        C.cst = take(512)
        C.meta = take(128)
        C.lng = take(96)
        C.lnb = take(96)
        C.lnga = take(96)
        C.lnba = take(96)
        C.psc = take(16)
        C.bf8 = take(128)
        C.corrw = take(64)
        C.mean = take(512)
        C.rstd = take(512)
        C.t0 = take(512)
        C.t1 = take(512)
        C.halo = [take(256).rearrange("p (c t) -> p c t", c=NCH) for _ in range(2)]
        C.LF = take(256).rearrange("p (b h) -> p b h", b=NBLK)
        C.SUF = take(256).rearrange("p (b h) -> p b h", b=NBLK)
        C.TT = take(16)
        C.PRE = take(64).rearrange("p (g h) -> p g h", g=4)
        bfc = take(448).bitcast(BF16)
        C.ident_b = bfc[:, 0:128]
        C.tri_b = bfc[:, 128:256]
        C.ones_b = bfc[:, 256:384]
        C.stg = [Buf(take(512), self.ds("stg%d" % i)) for i in range(3)]
        C.stgb = [Buf(take(256).bitcast(BF16), self.ds("stgb%d" % i)) for i in range(3)]
        assert o <= MISC, o
        C.ident = C.cst[:, 0:128]
        C.tri = C.cst[:, 128:256]
        C.lstrict = C.cst[:, 256:384]
        C.ones = C.cst[:, 384:512]
        self.tokstg = [Buf(self.f32view(TOKSTG + i * 2 * D, D), self.ds("tokstg%d" % i)) for i in range(2)]

        sd = self.ds("setup")
        ld = None
        for dst, src in ((C.cst, self.I.cst), (C.meta, self.I.meta), (C.lng, self.I.lng), (C.lnb, self.I.lnb),
                         (C.psc, self.I.psc), (C.bf8[:, 0:16], self.I.bf)):
            ld = P.dma("sync", dst, src, dsem=sd)
        ops = []
        ops.append(self.vts(C.lnga, C.lng, ALPHA, None, ALU.mult, deps=[ld]))
        ops.append(self.vts(C.lnba, C.lnb, ALPHA, None, ALU.mult, deps=[ld]))
        ops.append(self.vts(C.psc, C.psc, 1.0 / ALPHA, None, ALU.mult, deps=[ld]))
        for i in range(1, 8):
            ops.append(self.vcopy(C.bf8[:, 16 * i:16 * i + 16], C.bf8[:, 0:16], deps=[ld]))
        ops.append(self.vcopy(C.ident_b, C.ident, deps=[ld]))
        ops.append(self.vcopy(C.tri_b, C.tri, deps=[ld]))
        ops.append(self.vcopy(C.ones_b, C.ones, deps=[ld]))
        for g in range(4):
            ops.append(self.vts(C.corrw[:, 16 * g:16 * g + 16], C.meta[:, 32 + 16 * g:48 + 16 * g],
                                -1.0, 1.0 / (2 ** (g + 1)), ALU.add, ALU.mult, deps=[ld]))
        P.barrier()
        b = self.banks
        C.psA = Rot([b[0], b[1]])
        C.psB = Rot([b[2], b[3]])
        C.psC = Rot([b[4], b[5]])
        C.psD = Rot([b[6], b[7]])
        C.stgr = Rot(C.stg)
        C.stgbr = Rot(C.stgb)

    def tile_ctx(self, T, segs):
        c = Ctx()
        c.T = T
        c.segs = segs
        o = 0
        c.xbf = self.arena[:, o:o + NCH * T].rearrange("p (c t) -> p c t", c=NCH)
        o += NCH * T
        c.acc = self.arena[:, o:o + 2 * NCH * T].bitcast(F32).rearrange("p (c t) -> p c t", c=NCH)
        o += 2 * NCH * T
        c.hT = self.arena[:, o:o + NCH * T].rearrange("p (c t) -> p c t", c=NCH)
        o += NCH * T
        c.end = o
        c.xready = None
        c.accready = None
        return c

    def ffn(self, c, i):
        ws, C = self.ws, self.C
        k1, k3, k2 = "w1_%d" % i, "w3_%d" % i, "w2_%d" % i
        tmp = [Buf(C.t0), Buf(C.t1)]
        ti = 0
        last_acc = c.accready
        last_h = None
        nw = 512 // WC
        mpt = WC // 128
        for qd in range(DFFQ):
            for wt in range(2048 // WC):
                cb = qd * 2048 + wt * WC
                v1, d1, s1 = ws.get(self.wsrc(k1, 0, D, cb, WC), 16, WC, k1)
                v3, d3, s3 = ws.get(self.wsrc(k3, 0, D, cb, WC), 16, WC, k3)
                lastA = lastB = None
                for mi in range(mpt):
                    m = wt * mpt + mi
                    for (off, n) in c.segs:
                        A = C.psA.next()
                        B = C.psB.next()
                        for kc in range(16):
                            lastA = self.mm(A.ap[:, 0:n], None if v1 is None else v1[:, kc, mi * 128:(mi + 1) * 128],
                                            c.xbf[:, kc, off:off + n], kc == 0, kc == 15,
                                            deps=[d1, c.xready] + A.readers if kc == 0 else ())
                        A.readers = []
                        for kc in range(16):
                            lastB = self.mm(B.ap[:, 0:n], None if v3 is None else v3[:, kc, mi * 128:(mi + 1) * 128],
                                            c.xbf[:, kc, off:off + n], kc == 0, kc == 15,
                                            deps=[d3] + B.readers if kc == 0 else ())
                        B.readers = []
                        tb = tmp[ti % 2]
                        ti += 1
                        a_op = self.act(tb.ap[:, 0:n], A.ap[:, 0:n], AF.Silu, deps=[lastA] + tb.readers)
                        tb.readers = []
                        A.readers.append(a_op)
                        h_op = self.vtt(c.hT[:, m, off:off + n], tb.ap[:, 0:n], B.ap[:, 0:n], ALU.mult,
                                        deps=[a_op, lastB])
                        tb.readers.append(h_op)
                        B.readers.append(h_op)
                        last_h = h_op
                ws.release(s1, lastA)
                ws.release(s3, lastB)
            for wt in range(2048 // WC):
                v2, d2, s2 = ws.get(self.wsrc(k2, qd * 2048, 2048, wt * WC, WC), 16, WC, k2)
                lastC = None
                for ci in range(mpt):
                    cc = wt * mpt + ci
                    for (off, n) in c.segs:
                        Cb = C.psC.next()
                        for kc in range(16):
                            lastC = self.mm(Cb.ap[:, 0:n], None if v2 is None else v2[:, kc, ci * 128:(ci + 1) * 128],
                                            c.hT[:, kc, off:off + n], kc == 0, kc == 15,
                                            deps=[d2, last_h] + Cb.readers if kc == 0 else ())
                        Cb.readers = []
                        last_acc = self.vstt(c.acc[:, cc, off:off + n], Cb.ap[:, 0:n], 0.5,
                                             c.acc[:, cc, off:off + n], ALU.mult, ALU.add,
                                             deps=[lastC, last_acc])
                        Cb.readers.append(last_acc)
                ws.release(s2, lastC)
        c.accready = last_acc
        c.xready = None

    def ln(self, c, idx, final=False):
        C = self.C
        sq = c.hT
        last = c.accready
        lastact = None
        for (off, n) in c.segs:
            xb_op = self.vcopy(c.xbf[:, :, off:off + n], c.acc[:, :, off:off + n], deps=[last])
            sq_op = self.act(sq[:, :, off:off + n], c.acc[:, :, off:off + n], AF.Square, deps=[last])
            D0 = C.psD.next()
            D1 = C.psD.next()
            l0 = l1 = None
            for k in range(NCH):
                l0 = self.mm(D0.ap[:, 0:n], C.ones_b, c.xbf[:, k, off:off + n], k == 0, k == NCH - 1,
                             deps=[xb_op] + D0.readers if k == 0 else ())
            D0.readers = []
            for k in range(NCH):
                l1 = self.mm(D1.ap[:, 0:n], C.ones_b, sq[:, k, off:off + n], k == 0, k == NCH - 1,
                             deps=[sq_op] + D1.readers if k == 0 else ())
            D1.readers = []
            mean = C.mean[:, 0:n]
            rstd = C.rstd[:, 0:n]
            m_op = self.vts(mean, D0.ap[:, 0:n], 1.0 / D, None, ALU.mult, deps=[l0, xb_op])
            D0.readers.append(m_op)
            msq = self.vtt(C.t0[:, 0:n], mean, mean, ALU.mult, deps=[m_op])
            v_op = self.vstt(rstd, D1.ap[:, 0:n], 1.0 / D, C.t0[:, 0:n], ALU.mult, ALU.subtract,
                             deps=[l1, msq])
            D1.readers.append(v_op)
            r1 = self.vts(rstd, rstd, EPS, None, ALU.add, deps=[v_op])
            r2 = self.act(rstd, rstd, AF.Sqrt, deps=[r1])
            r_op = self.P.add("vector", lambda e, rstd=rstd: e.reciprocal(rstd, rstd), [r2])
            n_op = self.vstt(mean, mean, -1.0, rstd, ALU.mult, ALU.mult, deps=[r_op])
            prev = n_op
            for k in range(NCH):
                a = c.acc[:, k, off:off + n]
                o1 = self.vtt(a, a, rstd, ALU.mult, deps=[prev])
                o2 = self.vtt(a, a, mean, ALU.add, deps=[o1])
                gi = idx * NCH + k
                if final:
                    prev = self.vts(a, a, C.lng[:, gi:gi + 1], C.lnb[:, gi:gi + 1], ALU.mult, ALU.add,
                                    deps=[o2])
                else:
                    o3 = self.act(c.xbf[:, k, off:off + n], a, AF.Identity,
                                  bias=C.lnb[:, gi:gi + 1], scale=C.lng[:, gi:gi + 1], deps=[o2, l1])
                    prev = self.vts(a, a, C.lnga[:, gi:gi + 1], C.lnba[:, gi:gi + 1], ALU.mult, ALU.add,
                                    deps=[o3])
                    lastact = o3
            last = prev
        c.accready = last
        c.xready = [lastact, last]

    def load_tokens(self, c, pieces):
        C = self.C
        last_v = last_a = None
        for i, (rows, coff, n) in enumerate(pieces):
            sb = self.tokstg[i % 2]
            ld = self.P.dma("sync", sb.ap[0:n, :], rows, deps=list(sb.readers), dsem=sb.dsem)
            sb.readers = []
            for g in range(4):
                bk = C.psD.next()
                lt = None
                for j in range(4):
                    k = 4 * g + j
                    lt = self.tr(bk.ap[:, j * 128:j * 128 + n], sb.ap[0:n, k * 128:(k + 1) * 128],
                                 C.ident[0:n, 0:n], deps=[ld] + bk.readers if j == 0 else ())
                bk.readers = []
                src = bk.ap.rearrange("p (j t) -> p j t", j=4)[:, :, 0:n]
                last_a = self.acopy(c.acc[:, 4 * g:4 * g + 4, coff:coff + n], src, deps=[lt], scale=ALPHA)
                last_v = self.vcopy(c.xbf[:, 4 * g:4 * g + 4, coff:coff + n], src, deps=[lt, last_a])
                bk.readers += [last_a, last_v]
                sb.readers.append(lt)
        c.xready = [last_v]
        c.accready = [last_a, last_v]

    def store_tokens(self, c, pieces, scale=None):
        C = self.C
        last = None
        for i, (rows, coff, n) in enumerate(pieces):
            sb = self.tokstg[i % 2]
            evs = []
            for g in range(4):
                bk = C.psD.next()
                lt = None
                for j in range(4):
                    k = 4 * g + j
                    lt = self.tr(bk.ap[0:n, j * 128:(j + 1) * 128], c.acc[:, k, coff:coff + n],
                                 C.ident, deps=[c.accready] + bk.readers + sb.readers if j == 0 else ())
                bk.readers = []
                if g % 2 == 0:
                    ev = self.acopy(sb.ap[0:n, g * 512:(g + 1) * 512], bk.ap[0:n, :], deps=[lt], scale=scale)
                elif scale is None:
                    ev = self.vcopy(sb.ap[0:n, g * 512:(g + 1) * 512], bk.ap[0:n, :], deps=[lt])
                else:
                    ev = self.vts(sb.ap[0:n, g * 512:(g + 1) * 512], bk.ap[0:n, :], scale, None, ALU.mult,
                                  deps=[lt])
                bk.readers.append(ev)
                evs.append(ev)
            sb.readers = []
            if rows is not None:
                last = self.P.dma("sync", rows, sb.ap[0:n, :], deps=evs, dsem=sb.dsem)
                sb.readers.append(last)
                self.out_dmas.append(last)
            else:
                last = evs
        return last

    def pool_prompt(self, c, halo_in, halo_out, first, slot=0):
        C = self.C
        T = c.T
        L = T + 16
        ext = self.f32view(0, L)
        A = self.f32view(2 * L, L)
        Bb = self.f32view(4 * L, L)
        dbuf = c.hT
        hs = self.acopy(halo_out, c.acc[:, :, T - 16:T], deps=[c.accready])
        z1 = self.P.add("vector", lambda e: e.memset(A, 0.0), [c.accready, c.xready])
        z2 = self.P.add("vector", lambda e: e.memset(Bb, 0.0), [c.accready, c.xready])
        prev = [c.accready, c.xready, z1, z2]
        dop = None
        for k in range(NCH):
            g = k // 4
            w = 2 ** (g + 1)
            if first:
                e0 = self.vts(ext[:, 0:16], halo_in[:, k, :], C.meta[:, slot:slot + 1], None, ALU.mult,
                              deps=[prev, self.halo_ready])
            else:
                e0 = self.vcopy(ext[:, 0:16], halo_in[:, k, :], deps=[prev, self.halo_ready])
            e1 = self.acopy(ext[:, 16:L], c.acc[:, k, :], deps=[prev])
            src, dst = ext, A
            sh = 1
            dd = [e0, e1]
            lastop = None
            for s_ in range(g + 1):
                lastop = self.vtt(dst[:, sh:L], src[:, sh:L], src[:, 0:L - sh], ALU.add, deps=dd)
                dd = [lastop]
                src = dst
                dst = Bb if dst is A else A
                sh *= 2
            dop = self.vstt(dbuf[:, k, :], src[:, 16:L], 1.0 / w, ext[:, 16:L], ALU.mult, ALU.subtract,
                            deps=[lastop])
            if first:
                cw = self.vts(C.t1[:, 0:16], C.corrw[:, 16 * g:16 * g + 16], C.meta[:, 16 + slot:17 + slot], 1.0 / w,
                              ALU.mult, ALU.add, deps=[dop])
                t_ = self.vtt(C.t0[:, 0:16], src[:, 16:32], C.t1[:, 0:16], ALU.mult, deps=[dop, cw])
                dop = self.vtt(dbuf[:, k, 0:16], C.t0[:, 0:16], ext[:, 16:32], ALU.subtract, deps=[t_])
            prev = [dop]
        self.pool_mm(c, dbuf, dop, hs)
        return hs

    def pool_mm(self, c, dbuf, d_ready, extra):
        C, ws = self.C, self.ws
        last_acc = [d_ready, c.accready]
        mpt = WC // 128
        for g in range(4):
            for wt in range(512 // WC):
                v, dl, sl = ws.get(self.wsrc("pw", g * 512, 512, wt * WC, WC), 4, WC, "pw")
                lastm = None
                for oc in range(mpt):
                    k_out = 4 * g + wt * mpt + oc
                    for (off, n) in c.segs:
                        Cb = C.psC.next()
                        for kc in range(4):
                            lastm = self.mm(Cb.ap[:, 0:n], None if v is None else v[:, kc, oc * 128:(oc + 1) * 128],
                                            dbuf[:, 4 * g + kc, off:off + n], kc == 0, kc == 3,
                                            deps=[dl, d_ready] + Cb.readers if kc == 0 else ())
                        Cb.readers = []
                        last_acc = self.vstt(c.acc[:, k_out, off:off + n], Cb.ap[:, 0:n],
                                             C.psc[:, k_out:k_out + 1], c.acc[:, k_out, off:off + n],
                                             ALU.mult, ALU.add, deps=[lastm, last_acc, extra])
                        Cb.readers.append(last_acc)
                ws.release(sl, lastm)
        c.accready = last_acc

    def pool_sample(self, c):
        C, P = self.C, self.P
        o = c.end
        prevS = self.f32view(o, NCH * 60).rearrange("p (c t) -> p c t", c=NCH)
        o += 2 * NCH * 60
        R = NCH * NSEQ
        ext = self.f32view(o, R * 32).rearrange("p (r t) -> p r t", r=R)
        o += 2 * R * 32
        A = self.f32view(o, R * 32).rearrange("p (r t) -> p r t", r=R)
        o += 2 * R * 32
        Bb = self.f32view(o, R * 32).rearrange("p (r t) -> p r t", r=R)
        o += 2 * R * 32
        stg = self.tokstg[1]
        ld = P.dma("sync", stg.ap[0:60, :], self.I.sp, deps=list(stg.readers), dsem=stg.dsem)
        stg.readers = []
        evs = []
        for g in range(4):
            bk = C.psD.next()
            lt = None
            for j in range(4):
                k = 4 * g + j
                lt = self.tr(bk.ap[:, j * 64:j * 64 + 60], stg.ap[0:60, k * 128:(k + 1) * 128],
                             C.ident[0:60, 0:60], deps=[ld] + bk.readers if j == 0 else ())
            bk.readers = []
            src = bk.ap[:, 0:256].rearrange("p (j t) -> p j t", j=4)[:, :, 0:60]
            ev = self.acopy(prevS[:, 4 * g:4 * g + 4, :], src, deps=[lt], scale=ALPHA)
            bk.readers.append(ev)
            evs.append(ev)
            stg.readers.append(lt)
        ext4 = ext.rearrange("p (c s) t -> p c s t", c=NCH)
        lastop = None
        e_ops = []
        for s in range(NSEQ):
            e_ops.append(self.vcopy(ext4[:, :, s, 1:16], prevS[:, :, s * 15:(s + 1) * 15], deps=[evs[-1]]))
            e_ops.append(self.vcopy(ext4[:, :, s, 16:32], c.acc[:, :, s * 16:(s + 1) * 16], deps=[c.accready]))
            e_ops.append(self.vts(ext4[:, :, s, 0:1], c.acc[:, :, s * 16:s * 16 + 1], 0.0, None, ALU.mult,
                                  deps=[c.accready]))
        z1 = self.P.add("vector", lambda e: e.memset(A, 0.0), [c.accready])
        z2 = self.P.add("vector", lambda e: e.memset(Bb, 0.0), [c.accready])
        src, dst = ext, A
        sh = 1
        lastop = [e_ops[-1], z1, z2]
        res = {}
        for s_ in range(4):
            r0 = 16 * s_
            lastop = self.vtt(dst[:, r0:R, sh:32], src[:, r0:R, sh:32], src[:, r0:R, 0:32 - sh], ALU.add,
                              deps=[lastop])
            res[s_] = dst
            src = dst
            dst = Bb if dst is A else A
            sh *= 2
        dS = c.hT
        dop = lastop
        for g in range(4):
            w = 2 ** (g + 1)
            r4 = res[g].rearrange("p (c s) t -> p c s t", c=NCH)
            for s in range(NSEQ):
                dop = self.vstt(dS[:, 4 * g:4 * g + 4, s * 16:(s + 1) * 16], r4[:, 4 * g:4 * g + 4, s, 16:32],
                                1.0 / w, ext4[:, 4 * g:4 * g + 4, s, 16:32], ALU.mult, ALU.subtract, deps=[dop])
        return dop

    def store_pool_sample(self, c):
        evs = self.store_tokens(c, [(None, 0, 64)], scale=1.0 / ALPHA)
        sb = self.tokstg[0]
        for s in range(NSEQ):
            d = self.P.dma("sync", self.O.ps[s * 15:(s + 1) * 15, :], sb.ap[s * 16 + 1:s * 16 + 16, :],
                           deps=evs, dsem=sb.dsem)
            sb.readers.append(d)
            self.out_dmas.append(d)

    def store_pool_prompt(self, halo):
        C = self.C
        for g in range(4):
            sb = C.stgr.next()
            bk = C.psD.next()
            lt = None
            for j in range(4):
                k = 4 * g + j
                lt = self.tr(bk.ap[0:16, j * 128:(j + 1) * 128], halo[:, k, :], C.ident,
                             deps=[self.halo_ready] + bk.readers + sb.readers if j == 0 else ())
            bk.readers = []
            ev = self.acopy(sb.ap[0:16, :], bk.ap[0:16, :], deps=[lt], scale=1.0 / ALPHA)
            bk.readers.append(ev)
            d = self.P.dma("sync", self.O.pp[:, g * 512:(g + 1) * 512], sb.ap[1:16, :], deps=[ev], dsem=sb.dsem)
            sb.readers = [d]
            self.out_dmas.append(d)

    def proj_prompt(self, c, ti, slot, own):
        C, ws, P = self.C, self.ws, self.P
        O, S = self.O, self.S
        t0 = ti * TP
        kv0 = slot * PCH + t0
        nb = TP // 128
        hpt = WC // 128
        ntile = D // WC
        evi = [0]
        wd = "win"

        def evac(dst, src, deps):
            evi[0] += 1
            if evi[0] % 2:
                return self.acopy(dst, src, deps=deps)
            return self.vcopy(dst, src, deps=deps)

        def fm(v, dl, j, dst):
            lastm = None
            for hi in range(hpt):
                for (off, n) in c.segs:
                    bk = C.psA.next()
                    for kc in range(16):
                        lastm = self.mm(bk.ap[:, 0:n], None if v is None else v[:, kc, hi * 128:(hi + 1) * 128],
                                        c.xbf[:, kc, off:off + n], kc == 0, kc == 15,
                                        deps=[dl, c.xready] + bk.readers if kc == 0 else ())
                    bk.readers = []
                    sb = C.stgbr.next()
                    ev = evac(sb.ap[:, 0:n], bk.ap[:, 0:n], [lastm] + sb.readers)
                    bk.readers.append(ev)
                    r0 = (hpt * j + hi) * 128
                    cbase = kv0 if dst is S.KV else t0
                    d = P.dma("sync", dst[r0:r0 + 128, cbase + off:cbase + off + n], sb.ap[:, 0:n],
                              deps=[ev], dsem=sb.dsem)
                    sb.readers = [d]
                    self.spill_dmas.append(d)
            return lastm

        def tm(v, dl, j, out_ap, vscratch):
            lastm = None
            for b in range(nb):
                bk = C.psB.next()
                for kc in range(16):
                    lastm = self.mm(bk.ap[:, 0:WC], c.xbf[:, kc, b * 128:(b + 1) * 128],
                                    None if v is None else v[:, kc, :], kc == 0, kc == 15,
                                    deps=[dl, c.xready] + bk.readers if kc == 0 else ())
                bk.readers = []
                ev = None
                if out_ap is not None:
                    sb = C.stgr.next()
                    ev = evac(sb.ap[:, 0:WC], bk.ap[:, 0:WC], [lastm] + sb.readers)
                    bk.readers.append(ev)
                    d = P.dma("sync", out_ap[t0 + b * 128:t0 + (b + 1) * 128, j * WC:(j + 1) * WC], sb.ap[:, 0:WC],
                              deps=[ev], dsem=sb.dsem)
                    sb.readers = [d]
                    self.out_dmas.append(d)
                if vscratch:
                    sb2 = C.stgbr.next()
                    ev2 = evac(sb2.ap[:, 0:WC], bk.ap[:, 0:WC], [lastm, ev] + sb2.readers)
                    bk.readers.append(ev2)
                    blk = (kv0 // 128) + b
                    dst = S.KV[D + hpt * j * 128:D + hpt * (j + 1) * 128, blk * 128:(blk + 1) * 128] \
                        .rearrange("(h p) d -> p h d", p=128)
                    d2 = P.dma("sync", dst, sb2.ap[:, 0:WC].rearrange("p (h d) -> p h d", h=hpt), deps=[ev2],
                               dsem=sb2.dsem)
                    sb2.readers = [d2]
                    self.spill_dmas.append(d2)
            return lastm

        if own:
            for j in range(ntile):
                v, dl, sl = ws.get(self.wsrc("win", 0, D, j * WC, WC), 16, WC, wd)
                lm = fm(v, dl, j, S.QT)
                ws.release(sl, lm)
        for j in range(ntile):
            v, dl, sl = ws.get(self.wsrc("win", 0, D, D + j * WC, WC), 16, WC, wd)
            lm = fm(v, dl, j, S.KV)
            if own:
                lm = tm(v, dl, j, O.kp, False)
            ws.release(sl, lm)
        for j in range(ntile):
            v, dl, sl = ws.get(self.wsrc("win", 0, D, 2 * D + j * WC, WC), 16, WC, wd)
            lm = tm(v, dl, j, O.vp if own else None, True)
            ws.release(sl, lm)
        v, dl, sl = ws.get(self.wsrc("win", 0, D, 3 * D, H), 16, H, wd)
        bk = C.psC.next()
        lastm = None
        for b in range(nb):
            for kc in range(16):
                lastm = self.mm(bk.ap[:, b * 16:(b + 1) * 16], c.xbf[:, kc, b * 128:(b + 1) * 128],
                                None if v is None else v[:, kc, :], kc == 0, kc == 15,
                                deps=[dl, c.xready] + bk.readers + self.lf_readers if (kc == 0 and b == 0) else ())
        bk.readers = []
        ws.release(sl, lastm)
        lf = C.LF[:, ti * nb:(ti + 1) * nb, :]
        lf2 = C.LF.rearrange("p b h -> p (b h)")[:, ti * nb * 16:(ti + 1) * nb * 16]
        lop = self.logsig(lf2, bk.ap[:, 0:nb * 16], C.bf8[:, 0:nb * 16], nb * 16, 128, [lastm] + self.lf_readers, bk)
        self.lf_ready = lop
        if own:
            d = P.dma("sync", O.lp[t0:t0 + TP, :].rearrange("(b p) h -> p b h", p=128), lf, deps=[lop],
                      dsem=self.ds("lfout"))
            self.out_dmas.append(d)
            d = P.dma("sync", S.XR[:, t0:t0 + TP].rearrange("(c p) t -> p c t", p=128), c.acc,
                      deps=[c.accready], dsem=self.ds("xrspill"))
            self.spill_dmas.append(d)

    def logsig(self, out, zin, bias, n, npart, deps, bank=None):
        C = self.C
        z = C.t0[0:npart, 0:n]
        e = C.t1[0:npart, 0:n]
        o1 = self.vtt(z, zin, bias, ALU.add, deps=deps)
        if bank is not None:
            bank.readers.append(o1)
        o2a = self.vts(e, z, -1.0, None, ALU.mult, deps=[o1])
        o2 = self.vtt(e, e, z, ALU.max, deps=[o2a])
        o3 = self.act(e, e, AF.Exp, scale=-1.0, deps=[o2])
        o4 = self.act(e, e, AF.Ln, bias=1.0, deps=[o3])
        o5 = self.vts(z, z, 0.0, None, ALU.min, deps=[o4])
        o6 = self.vtt(out, z, e, ALU.subtract, deps=[o5])
        return o6

    def suffix_sums(self, slot):
        C, P = self.C, self.P
        bk = self.banks[6]
        bk2 = self.banks[7]
        lf = C.LF
        deps0 = [self.lf_ready] + bk.readers + bk2.readers
        last = None
        first = True
        for b in range(NBLK):
            n = NBLK - b
            for i, bb in enumerate(range(b, NBLK)):
                lhs = C.lstrict if bb == b else C.ones
                last = self.mm(bk.ap[:, b * 16:(b + 1) * 16], lhs, lf[:, bb, :], i == 0, i == n - 1,
                               deps=deps0 if first else ())
                first = False
        ev = self.vcopy(C.SUF.rearrange("p b h -> p (b h)"), bk.ap[:, 0:256], deps=[last] + self.lf_readers)
        for b in range(NBLK):
            last = self.mm(bk2.ap[:, 0:16], C.ones, lf[:, b, :], b == 0, b == NBLK - 1)
        for g in range(4):
            nb_ = 4 * g + 4
            for b in range(nb_):
                last = self.mm(bk2.ap[:, 16 + 16 * g:32 + 16 * g], C.ones, lf[:, b, :], b == 0, b == nb_ - 1)
        ev2 = self.vcopy(C.TT, bk2.ap[:, 0:16], deps=[last] + self.lf_readers)
        ev3 = self.vcopy(C.PRE.rearrange("p g h -> p (g h)"), bk2.ap[:, 16:80], deps=[last])
        bk.readers = [ev]
        bk2.readers = [ev2, ev3]
        r0 = slot * SFROWS
        d1 = P.dma("sync", self.S.SFG[r0:r0 + PCH, :].rearrange("(b p) h -> p b h", p=128), C.SUF, deps=[ev],
                   dsem=self.ds("sfspill"))
        d2 = P.dma("sync", self.S.SFG[r0 + PCH:r0 + PCH + 128, :], C.TT, deps=[ev2], dsem=self.ds("sfspill"))
        self.spill_dmas += [d1, d2]
        self.suf_ready = [ev, ev2, ev3]
        self.lf_readers = [last, d1, d2]

    def load_phase2(self, c, t0):
        P, S = self.P, self.S
        d1 = P.dma("sync", c.acc, S.XR[:, t0:t0 + TP].rearrange("(c p) t -> p c t", p=128),
                   deps=list(self.spill_dmas), dsem=self.ds("p2acc"))
        c.accready = d1
        if STAGE >= 3:
            d2 = P.dma("sync", c.xbf, S.OT[:, t0:t0 + TP].rearrange("(c p) t -> p c t", p=128),
                       deps=list(self.ot_dmas), dsem=self.ds("p2xbf"))
            c.xready = [d2]
        else:
            c.xready = None

    def wo(self, c):
        C, ws = self.C, self.ws
        last_acc = c.accready
        mpt = WC // 128
        for wt in range(D // WC):
            v, dl, sl = ws.get(self.wsrc("wo", 0, D, wt * WC, WC), 16, WC, "wo")
            lastm = None
            for ci in range(mpt):
                k_out = wt * mpt + ci
                for (off, n) in c.segs:
                    Cb = C.psC.next()
                    for kc in range(16):
                        lastm = self.mm(Cb.ap[:, 0:n], None if v is None else v[:, kc, ci * 128:(ci + 1) * 128],
                                        c.xbf[:, kc, off:off + n], kc == 0, kc == 15,
                                        deps=[dl, c.xready] + Cb.readers if kc == 0 else ())
                    Cb.readers = []
                    last_acc = self.vtt(c.acc[:, k_out, off:off + n], Cb.ap[:, 0:n], c.acc[:, k_out, off:off + n],
                                        ALU.add, deps=[lastm, last_acc])
                    Cb.readers.append(last_acc)
            ws.release(sl, lastm)
        c.accready = last_acc

    def proj_sample(self, c):
        C, ws, P = self.C, self.ws, self.P
        O = self.O
        o = c.end
        self.QTs = self.arena[:, o:o + H * 64].rearrange("p (h t) -> p h t", h=H)
        o += H * 64
        self.KTs = self.arena[:, o:o + H * 64].rearrange("p (h t) -> p h t", h=H)
        o += H * 64
        self.Vn = self.arena[0:16, o:o + NSEQ * D].rearrange("p (s d) -> p s d", s=NSEQ)
        o += NSEQ * D
        self.LFn = self.f32view(o, NSEQ * H)[0:16, :].rearrange("p (s h) -> p s h", s=NSEQ)
        o += 2 * NSEQ * H
        self.sa_base = o
        hpt = WC // 128
        wd = "win"
        lastev = None
        self.vn_ready = None
        for part, dstT in ((0, self.QTs), (1, self.KTs)):
            for j in range(D // WC):
                v, dl, sl = ws.get(self.wsrc("win", 0, D, part * D + j * WC, WC), 16, WC, wd)
                lastm = None
                for hi in range(hpt):
                    bk = C.psA.next()
                    for kc in range(16):
                        lastm = self.mm(bk.ap[:, 0:64], None if v is None else v[:, kc, hi * 128:(hi + 1) * 128],
                                        c.xbf[:, kc, 0:64], kc == 0, kc == 15,
                                        deps=[dl, c.xready] + bk.readers if kc == 0 else ())
                    bk.readers = []
                    ev = self.acopy(dstT[:, hpt * j + hi, :], bk.ap[:, 0:64], deps=[lastm])
                    bk.readers.append(ev)
                    lastev = ev
                if part == 1:
                    lastm = self.tm_sample(c, v, dl, j, O.ks, None)
                ws.release(sl, lastm)
        for j in range(D // WC):
            v, dl, sl = ws.get(self.wsrc("win", 0, D, 2 * D + j * WC, WC), 16, WC, wd)
            lastm = self.tm_sample(c, v, dl, j, O.vs, self.Vn)
            ws.release(sl, lastm)
        v, dl, sl = ws.get(self.wsrc("win", 0, D, 3 * D, H), 16, H, wd)
        bk = C.psC.next()
        lastm = None
        for s in range(NSEQ):
            for kc in range(16):
                lastm = self.mm(bk.ap[0:16, s * 16:(s + 1) * 16], c.xbf[:, kc, s * 16:(s + 1) * 16],
                                None if v is None else v[:, kc, :], kc == 0, kc == 15,
                                deps=[dl, c.xready] + bk.readers if (kc == 0 and s == 0) else ())
        bk.readers = []
        ws.release(sl, lastm)
        lop = self.logsig(self.LFn.rearrange("p s h -> p (s h)"), bk.ap[0:16, 0:64], C.bf8[0:16, 0:64], 64, 16,
                          [lastm], bk)
        d = P.dma("sync", O.ls.rearrange("(s p) h -> p s h", p=16), self.LFn, deps=[lop], dsem=self.ds("lsout"))
        self.out_dmas.append(d)
        self.sproj_ready = [lop, lastev, self.vn_ready]

    def tm_sample(self, c, v, dl, j, out_ap, vn):
        C, P = self.C, self.P
        lastm = None
        for s in range(NSEQ):
            bk = C.psB.next()
            for kc in range(16):
                lastm = self.mm(bk.ap[0:16, 0:WC], c.xbf[:, kc, s * 16:(s + 1) * 16],
                                None if v is None else v[:, kc, :], kc == 0, kc == 15,
                                deps=[dl, c.xready] + bk.readers if kc == 0 else ())
            bk.readers = []
            sb = C.stgr.next()
            ev = self.acopy(sb.ap[0:16, 0:WC], bk.ap[0:16, 0:WC], deps=[lastm] + sb.readers)
            bk.readers.append(ev)
            d = P.dma("sync", out_ap[s * 16:(s + 1) * 16, j * WC:(j + 1) * WC], sb.ap[0:16, 0:WC], deps=[ev],
                      dsem=sb.dsem)
            sb.readers = [d]
            self.out_dmas.append(d)
            if vn is not None:
                ev2 = self.vcopy(vn[:, s, j * WC:(j + 1) * WC], bk.ap[0:16, 0:WC], deps=[lastm, ev])
                bk.readers.append(ev2)
                self.vn_ready = ev2
        return lastm

    def attn_sample(self, c):
        C, P = self.C, self.P
        I = self.I
        o = self.sa_base
        kb = Buf(self.arena[:, o:o + 8192].rearrange("p (b n) -> p b n", b=16), self.ds("kc"))
        o += 8192
        vc_b = []
        for i in range(1):
            vc_b.append(Buf(self.arena[:, o:o + 8192].rearrange("p (b n) -> p b n", b=16), self.ds("vc%d" % i)))
            o += 8192
        ktc = []
        for i in range(2):
            ktc.append(Buf(self.arena[:, o:o + 4 * 2048].rearrange("p (h t) -> p h t", h=4)))
            o += 8192
        lfc = Buf(self.f32view(o, 256).rearrange("p (b h) -> p b h", b=16), self.ds("lfc"))
        o += 512
        Rb = self.f32view(o, 17 * 16).rearrange("p (b h) -> p b h", b=17)
        o += 2 * 17 * 16
        pts = [Buf(self.arena[:, o + 16 * i:o + 16 * i + 16]) for i in range(4)]
        o += 64
        rcp = self.f32view(o, 16)
        o += 32
        assert o <= TOKSTG, o
        oT = c.hT
        ptr = Rot(pts)
        gi = 0
        last_norm = None
        for s in range(NSEQ):
            ld = P.dma("sync", lfc.ap, I.cl[s].rearrange("(b p) h -> p b h", p=128),
                       deps=list(lfc.readers), dsem=lfc.dsem)
            lfc.readers = []
            bk = self.banks[6]
            last = None
            first = True
            for b in range(16):
                for i, bb in enumerate(range(b, 16)):
                    lhs = C.lstrict if bb == b else C.ones
                    last = self.mm(bk.ap[:, b * 16:(b + 1) * 16], lhs, lfc.ap[:, bb, :], i == 0, False,
                                   deps=[ld] + bk.readers + self.sproj_ready if first else ())
                    first = False
                last = self.mm(bk.ap[:, b * 16:(b + 1) * 16], C.ones[0:16, :], self.LFn[:, s, :], False, True)
            last = self.mm(bk.ap[0:16, 256:272], C.lstrict[0:16, 0:16], self.LFn[:, s, :], True, True)
            bk.readers = []
            lfc.readers.append(last)
            rdeps = [last, last_norm]
            ev = self.vcopy(Rb.rearrange("p b h -> p (b h)")[:, 0:256], bk.ap[:, 0:256], deps=rdeps)
            ev2 = self.vcopy(Rb[0:16, 16, :], bk.ap[0:16, 256:272], deps=rdeps)
            bk.readers += [ev, ev2]
            r_ready = [ev, ev2]
            for hg in range(4):
                vb = vc_b[0]
                kt = ktc[gi % 2]
                gi += 1
                dk = P.dma("gpsimd", kb.ap, I.ck[s, :, hg * 512:(hg + 1) * 512].rearrange("(b p) n -> p b n", p=128),
                           deps=list(kb.readers), dsem=kb.dsem)
                kb.readers = []
                dv = P.dma("gpsimd", vb.ap, I.cv[s, :, hg * 512:(hg + 1) * 512].rearrange("(b p) n -> p b n", p=128),
                           deps=list(vb.readers), dsem=vb.dsem)
                vb.readers = []
                kt_last = None
                for hi in range(4):
                    for q4 in range(4):
                        bk2 = C.psA.next()
                        bv = bk2.ap.bitcast(BF16)
                        lt = None
                        for j in range(4):
                            b = q4 * 4 + j
                            lt = self.tr(bv[:, j * 128:(j + 1) * 128], kb.ap[:, b, hi * 128:(hi + 1) * 128],
                                         C.ident_b, deps=[dk] + bk2.readers + kt.readers if j == 0 else ())
                        bk2.readers = []
                        evk = self.vcopy(kt.ap[:, hi, q4 * 512:(q4 + 1) * 512], bv[:, 0:512], deps=[lt])
                        bk2.readers.append(evk)
                        kt_last = evk
                        kb.readers.append(lt)
                kt.readers = []
                m3 = None
                for hi in range(4):
                    h = hg * 4 + hi
                    q = self.QTs[:, h, s * 16:(s + 1) * 16]
                    Ob = C.psC.next()
                    Sb = C.psD.next()
                    m1 = None
                    for b in range(17):
                        SB = C.psB.next()
                        if b < 16:
                            m1 = self.mm(SB.ap[:, 0:16], kt.ap[:, hi, b * 128:(b + 1) * 128], q, True, True,
                                         deps=[kt_last] + SB.readers + self.sproj_ready)
                            np_ = 128
                        else:
                            m1 = self.mm(SB.ap[0:16, 0:16], self.KTs[:, h, s * 16:(s + 1) * 16], q, True, True,
                                         deps=SB.readers + self.sproj_ready)
                            np_ = 16
                        SB.readers = []
                        pt = ptr.next()
                        e1 = self.act(pt.ap[0:np_, :], SB.ap[0:np_, 0:16], AF.Exp, bias=Rb[0:np_, b, h:h + 1],
                                      scale=SCALE, deps=[m1] + pt.readers + r_ready)
                        SB.readers.append(e1)
                        pt.readers = []
                        if b == 16:
                            e1 = self.vtt(pt.ap[0:16, :], pt.ap[0:16, :], C.tri_b[0:16, 0:16], ALU.mult, deps=[e1])
                            vv = self.Vn[:, s, h * 128:(h + 1) * 128]
                            on = C.ones_b[0:16, :]
                        else:
                            vv = vb.ap[:, b, hi * 128:(hi + 1) * 128]
                            on = C.ones_b
                        self.mm(Ob.ap[:, 0:16], vv, pt.ap[0:np_, :], b == 0, b == 16,
                                deps=[e1, dv] + (Ob.readers if b == 0 else []))
                        m3 = self.mm(Sb.ap[:, 0:16], on, pt.ap[0:np_, :], b == 0, b == 16,
                                     deps=(Sb.readers if b == 0 else []))
                        pt.readers.append(m3)
                    Ob.readers = []
                    Sb.readers = []
                    n1 = self.P.add("vector", lambda e, Sb=Sb: e.reciprocal(rcp, Sb.ap[:, 0:16]), [m3, last_norm])
                    n2 = self.vtt(oT[:, h, s * 16:(s + 1) * 16], Ob.ap[:, 0:16], rcp, ALU.mult, deps=[n1])
                    Ob.readers.append(n2)
                    Sb.readers.append(n1)
                    last_norm = n2
                    kt.readers.append(m1)
                vb.readers.append(m3)
        mv = self.vcopy(c.xbf[:, :, 0:64], oT[:, :, 0:64], deps=[last_norm])
        c.xready = [mv]

    def gather(self):
        self.kv_gathered = None

    def attn_prompt(self):
        C, P, S, I = self.C, self.P, self.S, self.I
        P.barrier()
        o = 0
        kvr = []
        for i in range(3):
            kvr.append(Buf(self.arena[:, o:o + 4096], self.ds("kvr%d" % i)))
            o += 4096
        qb = []
        for i in range(2):
            qb.append(Buf(self.arena[:, o:o + 2048], self.ds("qb%d" % i)))
            o += 2048
        pts = []
        for i in range(4):
            pts.append(Buf(self.arena[:, o:o + 512]))
            o += 512
        ost = []
        for i in range(2):
            ost.append(Buf(self.arena[:, o:o + 512], self.ds("ost%d" % i)))
            o += 512
        SUFA = self.f32view(o, 8 * 256).rearrange("p (r b h) -> p r b h", r=8, b=16)
        o += 2 * 8 * 256
        X = self.f32view(o, 8 * 64).rearrange("p (r g h) -> p r g h", r=8, g=4)
        o += 2 * 8 * 64
        Bh = []
        for i in range(2):
            Bh.append(Buf(self.f32view(o, 8 * 64).rearrange("p (r b g) -> p r b g", r=8, b=16)))
            o += 2 * 8 * 64
        rcp = self.f32view(o, 512)
        o += 1024
        ttm = self.f32view(o, 16)[0:8, :]
        o += 32
        wsl = self.f32view(o, 1024)[0:8, :]
        o += 2048
        assert o <= ARENA
        gd = list(self.spill_dmas)
        sd = self.ds("attsetup")
        lds = []
        for r in range(8):
            lds.append(P.dma("sync", SUFA[:, r], S.SFG[r * SFROWS:r * SFROWS + PCH, :]
                             .rearrange("(b p) h -> p b h", p=128), deps=[gd], dsem=sd))
        lds.append(P.dma("sync", ttm, S.SFG.rearrange("(r q) h -> r q h", r=8)[:, PCH, :], deps=[gd], dsem=sd))
        lds.append(P.dma("sync", wsl, I.wsel, dsem=sd))
        yb = self.banks[7]
        last = None
        for r in range(8):
            last = self.mm(yb.ap[:, r * 16:(r + 1) * 16], wsl[:, r * 128:(r + 1) * 128], ttm, True, True,
                           deps=[lds[-1]] + yb.readers if r == 0 else ())
        yb.readers = []
        xo = None
        for r in range(7):
            for g in range(4):
                xo = self.vstt(X[:, r, g, :], yb.ap[:, r * 16:(r + 1) * 16], C.meta[:, 8 + r:9 + r], C.PRE[:, g, :],
                               ALU.add, ALU.add, deps=[last, self.suf_ready])
        yb.readers.append(xo)
        for g in range(4):
            xo = self.vtt(X[:, 7, g, :], C.PRE[:, g, :], C.TT, ALU.subtract, deps=[self.suf_ready])
        x_ready = xo
        NSL = int(os.environ.get("MK_NSL", "8"))
        rlist = list(range(8 - NSL, 8))

        S0, S1 = self.banks[0], self.banks[1]
        S2, S3 = self.banks[2], self.banks[3]
        srot = Rot([S0, S1, S2, S3])
        prot = Rot(pts)
        accs = [(self.banks[4], self.banks[5]), (self.banks[6], self.banks[7])]
        kvi = 0
        self.ot_dmas = []
        ei = 0
        for h in range(H):
            q = qb[h % 2]
            qd = P.dma("sync", q.ap, S.QT[h * 128:(h + 1) * 128, :], deps=list(q.readers) + [self.spill_dmas],
                       dsem=q.dsem)
            q.readers = []
            bh = Bh[h % 2]
            bo = None
            for r in rlist:
                sfr = SUFA[:, r, :, h] if r < 7 else C.SUF[:, :, h]
                for g in range(4):
                    bo = self.vts(bh.ap[:, r, :, g], sfr, X[:, r, g, h:h + 1], None, ALU.add,
                                  deps=[x_ready, lds[0:8], bh.readers] if (r == rlist[0] and g == 0) else ())
            bh.readers = []
            for gp in range(2):
                first = [True, True]
                lastmm = [None, None]
                for r in rlist:
                    kv = kvr[kvi % 3]
                    kvi += 1
                    ksrc = S.KV[h * 128:(h + 1) * 128, r * PCH:(r + 1) * PCH]
                    vsrc = S.KV[D + h * 128:D + (h + 1) * 128, r * PCH:(r + 1) * PCH]
                    kdep = [gd]
                    kd = P.dma("sync", kv.ap[:, 0:2048], ksrc, deps=list(kv.readers) + kdep, dsem=kv.dsem)
                    vd = P.dma("sync", kv.ap[:, 2048:4096], vsrc, deps=[], dsem=kv.dsem)
                    kv.readers = []
                    KT = kv.ap[:, 0:2048]
                    V = kv.ap[:, 2048:4096].rearrange("p (b d) -> p b d", b=16)
                    lm = None
                    for blk in range(NBLK):
                        for gi_ in range(2):
                            G = 2 * gp + gi_
                            OTb, SMb = accs[gi_]
                            c0 = 0
                            diag = False
                            if r == 7:
                                j = blk - 4 * G
                                if j > 3:
                                    continue
                                if j >= 0:
                                    c0 = j * 128
                                    diag = True
                            islast = (r == 7 and blk == 4 * G + 3)
                            SB = srot.next()
                            m1 = self.mm(SB.ap[:, c0:512], KT[:, blk * 128:(blk + 1) * 128],
                                         q.ap[:, G * 512 + c0:(G + 1) * 512], True, True,
                                         deps=[kd, vd, qd] + SB.readers)
                            SB.readers = []
                            pt = prot.next()
                            e1 = self.act(pt.ap[:, c0:512], SB.ap[:, c0:512], AF.Exp, bias=bh.ap[:, r, blk, G:G + 1],
                                          scale=SCALE, deps=[m1, bo] + pt.readers)
                            SB.readers.append(e1)
                            pt.readers = []
                            if diag:
                                e1 = self.vtt(pt.ap[:, c0:c0 + 128], pt.ap[:, c0:c0 + 128], C.tri_b, ALU.mult,
                                              deps=[e1])
                            self.mm(OTb.ap[:, c0:512], V[:, blk, :], pt.ap[:, c0:512], first[gi_], islast,
                                    deps=[e1] + (OTb.readers if first[gi_] else []))
                            lm = self.mm(SMb.ap[:, c0:512], C.ones_b, pt.ap[:, c0:512], first[gi_], islast,
                                         deps=(SMb.readers if first[gi_] else []))
                            if first[gi_]:
                                OTb.readers = []
                                SMb.readers = []
                            first[gi_] = False
                            pt.readers.append(lm)
                            lastmm[gi_] = lm
                    kv.readers.append(lm)
                    bh.readers.append(lm)
                q.readers.append(lastmm[1])
                for gi_ in range(2):
                    G = 2 * gp + gi_
                    OTb, SMb = accs[gi_]
                    n1 = self.P.add("vector", lambda e, SMb=SMb: e.reciprocal(rcp, SMb.ap), [lastmm[gi_]])
                    ob = ost[ei % 2]
                    ei += 1
                    n2 = self.vtt(ob.ap, OTb.ap, rcp, ALU.mult, deps=[n1] + ob.readers)
                    OTb.readers.append(n2)
                    SMb.readers.append(n1)
                    d = P.dma("sync", S.OT[h * 128:(h + 1) * 128, G * 512:(G + 1) * 512], ob.ap, deps=[n2],
                              dsem=ob.dsem)
                    ob.readers = [d]
                    self.ot_dmas.append(d)
        P.barrier()

    def emit(self):
        P = self.P
        self.out_dmas = []
        self.spill_dmas = []
        self.ot_dmas = []
        self.halo_ready = None
        self.setup()
        self.prep_weights()
        C = self.C
        I, O = self.I, self.O

        def fin():
            if not P.dry:
                P.barrier(("tensor", "vector", "scalar", "sync", "gpsimd"))
                P.add("sync", None, [d for d in self.out_dmas if d is not None])
        if STOP <= 1:
            return fin()

        self.lf_readers = []
        cs = self.tile_ctx(192, [(0, 192)])
        self.load_tokens(cs, [(I.xs[:, :], 0, 64), (I.xh[:, :], 64, 128)])
        if STOP <= 2:
            return fin()
        self.ffn(cs, 0)
        if STOP <= 3:
            return fin()
        self.ln(cs, 0)
        hsp = None
        for sl_ in range(8):
            hsp = P.dma("sync", self.S.HSC[:, sl_ * 256:(sl_ + 1) * 256].rearrange("p (c t) -> p c t", c=NCH),
                        cs.acc[:, :, 64 + 16 * sl_:80 + 16 * sl_], deps=[cs.accready], dsem=self.ds("hsc"))
        self.store_pool_sample(cs)
        if STOP <= 4:
            return fin()
        dS_ready = self.pool_sample(cs)
        cs.segs = [(0, 64)]
        self.pool_mm(cs, cs.hT, dS_ready, None)
        self.ln(cs, 1)
        self.ffn(cs, 1)
        self.ln(cs, 2)
        self.ffn(cs, 2)
        self.ln(cs, 3)
        if STOP <= 5:
            return fin()
        self.proj_sample(cs)
        if STOP <= 6:
            return fin()
        if STAGE >= 2:
            self.attn_sample(cs)
            self.wo(cs)
        self.ln(cs, 4)
        self.ffn(cs, 3)
        self.ln(cs, 5, final=True)
        self.store_tokens(cs, [(O.ys[:, :], 0, 64)])
        P.barrier()
        if STOP <= 7:
            return fin()

        NSL = int(os.environ.get("MK_NSL", "8"))
        for slot in range(8 - NSL, 8):
            own = (slot == 7)
            hin, hout = C.halo[0], C.halo[1]
            for ti in range(PCH // TP):
                c = self.tile_ctx(TP, [(0, 512), (512, 512)])
                t0 = ti * TP
                r0 = slot * PCH + t0
                self.load_tokens(c, [(I.xp[r0 + 128 * b:r0 + 128 * (b + 1), :], 128 * b, 128)
                                     for b in range(TP // 128)])
                self.ffn(c, 0)
                self.ln(c, 0)
                if ti == 0:
                    self.halo_ready = P.dma("sync", hin,
                                            self.S.HSC[:, slot * 256:(slot + 1) * 256]
                                            .rearrange("p (c t) -> p c t", c=NCH),
                                            deps=[hsp, self.halo_ready], dsem=self.ds("hld"))
                self.halo_ready = self.pool_prompt(c, hin, hout, first=(ti == 0), slot=slot)
                if own and ti == PCH // TP - 1:
                    self.store_pool_prompt(hout)
                hin, hout = hout, hin
                self.ln(c, 1)
                self.ffn(c, 1)
                self.ln(c, 2)
                self.ffn(c, 2)
                self.ln(c, 3)
                self.proj_prompt(c, ti, slot, own)
                P.barrier()
            self.suffix_sums(slot)
        P.barrier()
        if STAGE >= 3:
            self.gather()
            self.attn_prompt()
        for ti in range(NPT):
            c = self.tile_ctx(TP, [(0, 512), (512, 512)])
            t0 = ti * TP
            self.load_phase2(c, t0)
            if STAGE >= 3:
                self.wo(c)
            self.ln(c, 4)
            self.ffn(c, 3)
            self.ln(c, 5, final=True)
            self.store_tokens(c, [(O.yp[t0 + 128 * b:t0 + 128 * (b + 1), :], 128 * b, 128)
                                  for b in range(TP // 128)])
            P.barrier()
        if not P.dry:
            P.add("sync", None, [d for d in self.out_dmas if d is not None])


_NC_CACHE = {}


def _consts():
    c = np.zeros((128, 512), np.float32)
    i = np.arange(128)
    c[:, 0:128] = np.eye(128, dtype=np.float32)
    c[:, 128:256] = (i[:, None] <= i[None, :]).astype(np.float32)
    c[:, 256:384] = (i[:, None] > i[None, :]).astype(np.float32)
    c[:, 384:512] = 1.0
    return c


def kernel(x_prompt, x_sample, state_pool, cache_fox_k, cache_fox_v, cache_fox_logf,
           ln_g, ln_b, ffn_w1, ffn_w3, ffn_w2, pool_w, pool_scale, fox_w_in, fox_b_f, fox_w_o):
    f = np.float32
    xp_full = np.asarray(x_prompt, f)[0]
    xs_full = np.asarray(x_sample, f)
    sp_full = np.asarray(state_pool, f)[0]
    ck = np.asarray(cache_fox_k, f)[0].reshape(32, -1, D)
    cv = np.asarray(cache_fox_v, f)[0].reshape(32, -1, D)
    cl = np.asarray(cache_fox_logf, f)[0]
    lng = np.ascontiguousarray(np.asarray(ln_g, f).reshape(6, NCH, 128).transpose(2, 0, 1).reshape(128, 96))
    lnb = np.ascontiguousarray(np.asarray(ln_b, f).reshape(6, NCH, 128).transpose(2, 0, 1).reshape(128, 96))
    psc = np.ascontiguousarray(np.asarray(pool_scale, f)[0].reshape(NCH, 128).T)
    bfr = np.ascontiguousarray(np.broadcast_to(np.asarray(fox_b_f, f)[0][None, :], (128, H)))
    w1 = np.ascontiguousarray(np.asarray(ffn_w1, f).reshape(4, D, -1)[:, :, :DFFX])
    w3 = np.ascontiguousarray(np.asarray(ffn_w3, f).reshape(4, D, -1)[:, :, :DFFX])
    w2 = np.ascontiguousarray(np.asarray(ffn_w2, f).reshape(4, -1, D)[:, :DFFX, :])
    pw = np.asarray(pool_w, f)[0].reshape(D, 512)
    win = np.asarray(fox_w_in, f)[0]
    wo = np.asarray(fox_w_o, f)[0]
    cst = _consts()
    in_maps = []
    for r in range(NRUN):
        chunks = [(r + 1 + i) % NCORE for i in range(NCORE)]
        xp = np.concatenate([xp_full[c * PCH:(c + 1) * PCH] for c in chunks], axis=0)
        xh = np.zeros((128, D), f)
        meta = np.zeros((128, 128), f)
        for sl, c in enumerate(chunks):
            if c > 0:
                xh[16 * sl:16 * sl + 16] = xp_full[c * PCH - 16:c * PCH]
            meta[:, sl] = 1.0 if c > 0 else 0.0
            meta[:, 8 + sl] = 0.0 if c < r else NEG
            meta[:, 16 + sl] = 1.0 if c == 0 else 0.0
        for g in range(4):
            w = 2 ** (g + 1)
            for t in range(16):
                meta[:, 32 + 16 * g + t] = w / min(t + 1, w)
        wsel = np.zeros((8, 8, 128), f)
        for rr in range(7):
            for rp in range(7):
                if chunks[rr] < chunks[rp] < r:
                    wsel[rp, rr, :] = 1.0
        m = {
            "xp": xp, "xh": xh, "xs": np.ascontiguousarray(xs_full[4 * r:4 * r + 4].reshape(64, D)),
            "sp": np.ascontiguousarray(sp_full[4 * r:4 * r + 4].reshape(60, D)),
            "ck": np.ascontiguousarray(ck[4 * r:4 * r + 4, :CPAST]), "cv": np.ascontiguousarray(cv[4 * r:4 * r + 4, :CPAST]),
            "cl": np.ascontiguousarray(cl[4 * r:4 * r + 4, :CPAST]),
            "w1": w1, "w3": w3, "w2": w2, "pw": pw, "win": win, "wo": wo,
            "psc": psc, "bf": bfr, "lng": lng, "lnb": lnb, "cst": cst,
            "meta": meta, "wsel": np.ascontiguousarray(wsel.reshape(8, 1024)),
        }
        in_maps.append(m)
    if "nc" not in _NC_CACHE:
        _NC_CACHE["nc"] = build_program()
    nc = _NC_CACHE["nc"]
    res = run_bass_kernel_spmd(nc, in_maps, core_ids=list(range(NRUN)))
    R = list(res.results)
    while len(R) < NCORE:
        R.append({k: np.zeros_like(np.asarray(v)) for k, v in R[0].items()})
    cat = lambda k: np.concatenate([np.asarray(R[r][k], f) for r in range(NCORE)], axis=0)
    y_prompt = cat("yp").reshape(1, NCORE * PCH, D)
    y_sample = cat("ys").reshape(32, TSQ, D)
    pool_p = np.asarray(R[NCORE - 1]["pp"], f).reshape(1, 1, 15, D)
    pool_s = cat("ps").reshape(1, 32, 15, D)
    k_p = cat("kp").reshape(1, 1, NCORE * PCH, H, 128)
    v_p = cat("vp").reshape(1, 1, NCORE * PCH, H, 128)
    lf_p = cat("lp").reshape(1, 1, NCORE * PCH, H)
    k_s = cat("ks").reshape(1, 32, TSQ, H, 128)
    v_s = cat("vs").reshape(1, 32, TSQ, H, 128)
    lf_s = cat("ls").reshape(1, 32, TSQ, H)
    return (y_prompt, y_sample, pool_p, pool_s, k_p, v_p, lf_p, k_s, v_s, lf_s)
```

```python
import os
from contextlib import ExitStack

import numpy as np
import concourse.bass as bass
import concourse.mybir as mybir
from concourse.bass_utils import run_bass_kernel_spmd

F32 = mybir.dt.float32
BF16 = mybir.dt.bfloat16
ALU = mybir.AluOpType
AF = mybir.ActivationFunctionType

NCORE = 8
D = 2048
DFF = 8192
NCH = 16
H = 16
PCH = 2048
HALO = 16
NSEQ = 4
TSQ = 16
PAST = 2048
ALPHA = float(4 ** 0.25)
EPS = 1e-5
NEG = -30000.0
TP = 1024
NBLK = PCH // 128
SFROWS = PCH + 128
SCALE = float(128 ** -0.5)
WC = 256
WSLOT = 16 * WC
NSLOT = 6
ARENA = 66048
TOKSTG = 57856
MISC = 7168

STAGE = int(os.environ.get("MK_STAGE", "3"))
NPT = int(os.environ.get("MK_NPT", str(PCH // TP)))
STOP = int(os.environ.get("MK_STOP", "99"))
SMALL = bool(int(os.environ.get("MK_SMALL", "0")))
CPAST = 128 if SMALL else PAST
DFFQ = int(os.environ.get("MK_DFFQ", "4"))
DFFX = 2048 * DFFQ
NRUN = int(os.environ.get("MK_CORES", str(NCORE)))


class Op:
    __slots__ = ("eng", "fn", "deps", "sig", "tok", "dsem")

    def __init__(self, eng, fn, deps, dsem=None):
        self.eng = eng
        self.fn = fn
        self.deps = deps
        self.sig = False
        self.tok = None
        self.dsem = dsem


class DSem:
    def __init__(self, handle):
        self.h = handle
        self.count = 0


ENGS = ["tensor", "vector", "scalar", "gpsimd", "sync"]


def flat(deps):
    out = []
    for d in deps:
        if d is None:
            continue
        if isinstance(d, (list, tuple)):
            out.extend(flat(d))
        else:
            out.append(d)
    return out


class Prog:
    def __init__(self, nc, stack):
        self.nc = nc
        self.stack = stack
        self.q = {e: [] for e in ENGS}
        self.sem = {e: stack.enter_context(nc.semaphore("e_" + e)) for e in ENGS}
        self.dry = True
        self.last = {e: None for e in ENGS}
        self.nds = 0

    def dsem(self):
        self.nds += 1
        return DSem(self.stack.enter_context(self.nc.semaphore("d%d" % self.nds)))

    def add(self, eng, fn, deps=()):
        if self.dry:
            return None
        op = Op(eng, fn, flat(deps))
        self.q[eng].append(op)
        self.last[eng] = op
        return op

    def dma(self, eng, out, in_, deps=(), dsem=None):
        if self.dry:
            return None
        assert tuple(out.shape) == tuple(in_.shape) or out.size() == in_.size(), (out.shape, in_.shape)
        op = self.add(eng, lambda e: e.dma_start(out=out, in_=in_), deps)
        op.dsem = dsem
        return op

    def barrier(self, engs=("tensor", "vector", "scalar", "sync")):
        if self.dry:
            return
        lasts = [self.last[e] for e in engs if self.last[e] is not None]
        for e in engs:
            self.add(e, None, [l for l in lasts if l.eng != e])

    def flush(self, block):
        for e in ENGS:
            for op in self.q[e]:
                for d in op.deps:
                    d.sig = True
        for e in ENGS:
            c = 0
            for op in self.q[e]:
                if op.fn is None:
                    continue
                if op.dsem is not None:
                    op.dsem.count += 16
                    op.tok = (op.dsem.h, op.dsem.count)
                elif op.sig:
                    c += 1
                    op.tok = (self.sem[e], c)
            assert c < 60000, (e, c)
        nwait = {e: 0 for e in ENGS}

        def run(e, engine):
            waited = {}
            for op in self.q[e]:
                for d in op.deps:
                    t = d.tok
                    if t is None:
                        continue
                    if d.eng == "tensor" and e == "tensor" and d.dsem is None:
                        continue
                    key = id(t[0])
                    if waited.get(key, 0) >= t[1]:
                        continue
                    engine.wait_ge(t[0], t[1])
                    nwait[e] += 1
                    waited[key] = t[1]
                if op.fn is None:
                    continue
                ins = op.fn(engine)
                if op.tok is not None:
                    ins.then_inc(op.tok[0], 16 if op.dsem is not None else 1)

        @block.tensor
        def _(t):
            run("tensor", t)

        @block.vector
        def _(v):
            run("vector", v)

        @block.scalar
        def _(s):
            run("scalar", s)

        @block.gpsimd
        def _(g):
            run("gpsimd", g)

        @block.sync
        def _(s):
            run("sync", s)

        if os.environ.get("MK_VERBOSE"):
            print("ops:", {e: len(self.q[e]) for e in ENGS}, "waits:", nwait, flush=True)


class Bank:
    def __init__(self, ap):
        self.ap = ap
        self.readers = []


class Rot:
    def __init__(self, items):
        self.items = items
        self.i = 0

    def next(self):
        it = self.items[self.i % len(self.items)]
        self.i += 1
        return it


class Buf:
    def __init__(self, ap, dsem=None):
        self.ap = ap
        self.readers = []
        self.dsem = dsem


class WStream:
    def __init__(self, P, slots):
        self.P = P
        self.slots = slots
        self.plan = []
        self.reset()

    def reset(self):
        self.nissued = 0
        self.nget = 0
        self.nrel = 0
        self.loads = {}
        for s in self.slots:
            s.readers = []

    def _pump(self):
        n = len(self.slots)
        while self.nissued < len(self.plan) and self.nissued < self.nrel + n:
            j = self.nissued
            src, kc, ncol, key = self.plan[j]
            slot = self.slots[j % n]
            view = slot.ap[:, 0:kc * ncol].rearrange("p (k n) -> p k n", k=kc)
            op = self.P.dma("gpsimd", view, src, deps=list(slot.readers) + [self.wdep[key]], dsem=slot.dsem)
            slot.readers = []
            self.loads[j] = (view, op, slot)
            self.nissued += 1

    def get(self, src, kc, ncol, key):
        i = self.nget
        self.nget += 1
        if self.P.dry:
            self.plan.append([src, kc, ncol, key])
            return (None, None, None)
        assert self.plan[i][1] == kc and self.plan[i][2] == ncol and self.plan[i][3] == key
        self._pump()
        return self.loads.pop(i)

    def release(self, slot, op):
        if self.P.dry:
            return
        slot.readers.append(op)
        self.nrel += 1
        self._pump()


class Ctx:
    pass


def build_program():
    nc = bass.Bass("TRN2", target_bir_lowering=False)

    def din(name, shape, dt=F32):
        return nc.dram_tensor(name, list(shape), dt, kind="ExternalInput").ap()

    def dout(name, shape, dt=F32):
        return nc.dram_tensor(name, list(shape), dt, kind="ExternalOutput").ap()

    I = Ctx()
    I.xp = din("xp", [8 * PCH, D])
    I.xh = din("xh", [128, D])
    I.xs = din("xs", [NSEQ * TSQ, D])
    I.sp = din("sp", [NSEQ * 15, D])
    I.ck = din("ck", [NSEQ, CPAST, D])
    I.cv = din("cv", [NSEQ, CPAST, D])
    I.cl = din("cl", [NSEQ, CPAST, H])
    I.w1 = din("w1", [4, D, DFFX])
    I.w3 = din("w3", [4, D, DFFX])
    I.w2 = din("w2", [4, DFFX, D])
    I.pw = din("pw", [D, 512])
    I.win = din("win", [D, 3 * D + H])
    I.wo = din("wo", [D, D])
    I.psc = din("psc", [128, NCH])
    I.bf = din("bf", [128, H])
    I.lng = din("lng", [128, 6 * NCH])
    I.lnb = din("lnb", [128, 6 * NCH])
    I.cst = din("cst", [128, 512])
    I.meta = din("meta", [128, 128])
    I.wsel = din("wsel", [8, 8 * 128])

    O = Ctx()
    O.yp = dout("yp", [PCH, D])
    O.ys = dout("ys", [NSEQ * TSQ, D])
    O.pp = dout("pp", [15, D])
    O.ps = dout("ps", [NSEQ * 15, D])
    O.kp = dout("kp", [PCH, D])
    O.vp = dout("vp", [PCH, D])
    O.lp = dout("lp", [PCH, H])
    O.ks = dout("ks", [NSEQ * TSQ, D])
    O.vs = dout("vs", [NSEQ * TSQ, D])
    O.ls = dout("ls", [NSEQ * TSQ, H])

    S = Ctx()
    S.XR = nc.dram_tensor("XR", [D, PCH], F32).ap()
    S.QT = nc.dram_tensor("QT", [D, PCH], BF16).ap()
    S.OT = nc.dram_tensor("OT", [D, PCH], BF16).ap()
    S.KV = nc.dram_tensor("KV", [2 * D, 8 * PCH], BF16).ap()
    S.SFG = nc.dram_tensor("SFG", [8 * SFROWS, H], F32).ap()
    S.HSC = nc.dram_tensor("HSC", [128, 8 * 256], F32).ap()
    with ExitStack() as st:
        arena = st.enter_context(nc.sbuf_tensor("arena", [128, ARENA], BF16))
        wring = st.enter_context(nc.sbuf_tensor("wring", [128, NSLOT * WSLOT], BF16))
        misc = st.enter_context(nc.sbuf_tensor("misc", [128, MISC], F32))
        psum = [st.enter_context(nc.psum_tensor("ps%d" % i, [128, 512], F32)) for i in range(8)]
        P = Prog(nc, st)
        banks = [Bank(p[:]) for p in psum]
        ws = WStream(P, [Buf(wring[:, i * WSLOT:(i + 1) * WSLOT], P.dsem()) for i in range(NSLOT)])
        E = Emitter(nc, P, ws, I, O, S, arena[:], misc[:], banks)
        P.dry = True
        E.emit()
        ws.reset()
        P.dry = False
        E.emit()
        with nc.Block() as block:
            P.flush(block)
    return nc


class Emitter:
    def __init__(self, nc, P, ws, I, O, S, arena, misc, banks):
        self.nc, self.P, self.ws = nc, P, ws
        self.I, self.O, self.S = I, O, S
        self.arena, self.misc, self.banks = arena, misc, banks
        self.dsems = {}

    def ds(self, key):
        if key not in self.dsems:
            self.dsems[key] = self.P.dsem()
        return self.dsems[key]

    def f32view(self, off_bf, nelem_f32):
        return self.arena[:, off_bf:off_bf + 2 * nelem_f32].bitcast(F32)

    def mm(self, out, lhsT, rhs, start, stop, deps=()):
        return self.P.add("tensor", lambda e: e.matmul(out, lhsT, rhs, start=start, stop=stop), deps)

    def tr(self, out, in_, ident, deps=()):
        return self.P.add("tensor", lambda e: e.transpose(out, in_, ident), deps)

    def act(self, out, in_, func, bias=None, scale=None, deps=()):
        kw = {}
        if bias is not None:
            kw["bias"] = bias
        if scale is not None:
            kw["scale"] = scale
        return self.P.add("scalar", lambda e: e.activation(out, in_, func, **kw), deps)

    def vtt(self, out, a, b, op, deps=()):
        return self.P.add("vector", lambda e: e.tensor_tensor(out, a, b, op), deps)

    def vts(self, out, a, s1, s2, op0, op1=None, deps=()):
        if op1 is None:
            return self.P.add("vector", lambda e: e.tensor_scalar(out, a, s1, None, op0), deps)
        return self.P.add("vector", lambda e: e.tensor_scalar(out, a, s1, s2, op0, op1), deps)

    def vstt(self, out, a, s, b, op0, op1, deps=()):
        return self.P.add("vector", lambda e: e.scalar_tensor_tensor(out, a, s, b, op0, op1), deps)

    def vcopy(self, out, in_, deps=()):
        return self.P.add("vector", lambda e: e.tensor_copy(out, in_), deps)

    def acopy(self, out, in_, deps=(), scale=None):
        if scale is None:
            return self.P.add("scalar", lambda e: e.activation(out, in_, AF.Copy), deps)
        return self.P.add("scalar", lambda e: e.activation(out, in_, AF.Identity, scale=scale), deps)

    def wsrc(self, key, r0, nrows, c0, ncols):
        I = self.I
        if key[0:2] in ("w1", "w3", "w2"):
            src = {"w1": I.w1, "w3": I.w3, "w2": I.w2}[key[0:2]][int(key[3:])]
        else:
            src = {"pw": I.pw, "win": I.win, "wo": I.wo}[key]
        return src[r0:r0 + nrows, c0:c0 + ncols].rearrange("(k p) n -> p k n", p=128)

    def prep_weights(self):
        self.wdep = {}
        self.ws.wdep = self.wdep
        for i in range(4):
            for k in ("w1", "w3", "w2"):
                self.wdep["%s_%d" % (k, i)] = None
        for k in ("pw", "win", "wo"):
            self.wdep[k] = None

    def setup(self):
        P, m = self.P, self.misc
        C = Ctx()
        self.C = C
        o = 0

        def take(n):
            nonlocal o
            v = m[:, o:o + n]
            o += n
            return v

        C.cst = take(512)
        C.meta = take(128)
        C.lng = take(96)
        C.lnb = take(96)
        C.lnga = take(96)
        C.lnba = take(96)
        C.psc = take(16)
        C.bf8 = take(128)
        C.corrw = take(64)
        C.mean = take(512)
        C.rstd = take(512)
        C.t0 = take(512)
        C.t1 = take(512)
        C.halo = [take(256).rearrange("p (c t) -> p c t", c=NCH) for _ in range(2)]
        C.LF = take(256).rearrange("p (b h) -> p b h", b=NBLK)
        C.SUF = take(256).rearrange("p (b h) -> p b h", b=NBLK)
        C.TT = take(16)
        C.PRE = take(64).rearrange("p (g h) -> p g h", g=4)
        bfc = take(448).bitcast(BF16)
        C.ident_b = bfc[:, 0:128]
        C.tri_b = bfc[:, 128:256]
        C.ones_b = bfc[:, 256:384]
        C.stg = [Buf(take(512), self.ds("stg%d" % i)) for i in range(3)]
        C.stgb = [Buf(take(256).bitcast(BF16), self.ds("stgb%d" % i)) for i in range(3)]
        assert o <= MISC, o
        C.ident = C.cst[:, 0:128]
        C.tri = C.cst[:, 128:256]
        C.lstrict = C.cst[:, 256:384]
        C.ones = C.cst[:, 384:512]
        self.tokstg = [Buf(self.f32view(TOKSTG + i * 2 * D, D), self.ds("tokstg%d" % i)) for i in range(2)]

        sd = self.ds("setup")
        ld = None
        for dst, src in ((C.cst, self.I.cst), (C.meta, self.I.meta), (C.lng, self.I.lng), (C.lnb, self.I.lnb),
                         (C.psc, self.I.psc), (C.bf8[:, 0:16], self.I.bf)):
            ld = P.dma("sync", dst, src, dsem=sd)
        ops = []
        ops.append(self.vts(C.lnga, C.lng, ALPHA, None, ALU.mult, deps=[ld]))
        ops.append(self.vts(C.lnba, C.lnb, ALPHA, None, ALU.mult, deps=[ld]))
        ops.append(self.vts(C.psc, C.psc, 1.0 / ALPHA, None, ALU.mult, deps=[ld]))
        for i in range(1, 8):
            ops.append(self.vcopy(C.bf8[:, 16 * i:16 * i + 16], C.bf8[:, 0:16], deps=[ld]))
        ops.append(self.vcopy(C.ident_b, C.ident, deps=[ld]))
        ops.append(self.vcopy(C.tri_b, C.tri, deps=[ld]))
        ops.append(self.vcopy(C.ones_b, C.ones, deps=[ld]))
        for g in range(4):
            ops.append(self.vts(C.corrw[:, 16 * g:16 * g + 16], C.meta[:, 32 + 16 * g:48 + 16 * g],
                                -1.0, 1.0 / (2 ** (g + 1)), ALU.add, ALU.mult, deps=[ld]))
        P.barrier()
        b = self.banks
        C.psA = Rot([b[0], b[1]])
        C.psB = Rot([b[2], b[3]])
        C.psC = Rot([b[4], b[5]])
        C.psD = Rot([b[6], b[7]])
        C.stgr = Rot(C.stg)
        C.stgbr = Rot(C.stgb)

    def tile_ctx(self, T, segs):
        c = Ctx()
        c.T = T
        c.segs = segs
        o = 0
        c.xbf = self.arena[:, o:o + NCH * T].rearrange("p (c t) -> p c t", c=NCH)
        o += NCH * T
        c.acc = self.arena[:, o:o + 2 * NCH * T].bitcast(F32).rearrange("p (c t) -> p c t", c=NCH)
        o += 2 * NCH * T
        c.hT = self.arena[:, o:o + NCH * T].rearrange("p (c t) -> p c t", c=NCH)
        o += NCH * T
        c.end = o
        c.xready = None
        c.accready = None
        return c

    def ffn(self, c, i):
        ws, C = self.ws, self.C
        k1, k3, k2 = "w1_%d" % i, "w3_%d" % i, "w2_%d" % i
        tmp = [Buf(C.t0), Buf(C.t1)]
        ti = 0
        last_acc = c.accready
        last_h = None
        nw = 512 // WC
        mpt = WC // 128
        for qd in range(DFFQ):
            for wt in range(2048 // WC):
                cb = qd * 2048 + wt * WC
                v1, d1, s1 = ws.get(self.wsrc(k1, 0, D, cb, WC), 16, WC, k1)
                v3, d3, s3 = ws.get(self.wsrc(k3, 0, D, cb, WC), 16, WC, k3)
                lastA = lastB = None
                for mi in range(mpt):
                    m = wt * mpt + mi
                    for (off, n) in c.segs:
                        A = C.psA.next()
                        B = C.psB.next()
                        for kc in range(16):
                            lastA = self.mm(A.ap[:, 0:n], None if v1 is None else v1[:, kc, mi * 128:(mi + 1) * 128],
                                            c.xbf[:, kc, off:off + n], kc == 0, kc == 15,
                                            deps=[d1, c.xready] + A.readers if kc == 0 else ())
                        A.readers = []
                        for kc in range(16):
                            lastB = self.mm(B.ap[:, 0:n], None if v3 is None else v3[:, kc, mi * 128:(mi + 1) * 128],
                                            c.xbf[:, kc, off:off + n], kc == 0, kc == 15,
                                            deps=[d3] + B.readers if kc == 0 else ())
                        B.readers = []
                        tb = tmp[ti % 2]
                        ti += 1
                        a_op = self.act(tb.ap[:, 0:n], A.ap[:, 0:n], AF.Silu, deps=[lastA] + tb.readers)
                        tb.readers = []
                        A.readers.append(a_op)
                        h_op = self.vtt(c.hT[:, m, off:off + n], tb.ap[:, 0:n], B.ap[:, 0:n], ALU.mult,
                                        deps=[a_op, lastB])
                        tb.readers.append(h_op)
                        B.readers.append(h_op)
                        last_h = h_op
                ws.release(s1, lastA)
                ws.release(s3, lastB)
            for wt in range(2048 // WC):
                v2, d2, s2 = ws.get(self.wsrc(k2, qd * 2048, 2048, wt * WC, WC), 16, WC, k2)
                lastC = None
                for ci in range(mpt):
                    cc = wt * mpt + ci
                    for (off, n) in c.segs:
                        Cb = C.psC.next()
                        for kc in range(16):
                            lastC = self.mm(Cb.ap[:, 0:n], None if v2 is None else v2[:, kc, ci * 128:(ci + 1) * 128],
                                            c.hT[:, kc, off:off + n], kc == 0, kc == 15,
                                            deps=[d2, last_h] + Cb.readers if kc == 0 else ())
                        Cb.readers = []
                        last_acc = self.vstt(c.acc[:, cc, off:off + n], Cb.ap[:, 0:n], 0.5,
                                             c.acc[:, cc, off:off + n], ALU.mult, ALU.add,
                                             deps=[lastC, last_acc])
                        Cb.readers.append(last_acc)
                ws.release(s2, lastC)
        c.accready = last_acc
        c.xready = None

    def ln(self, c, idx, final=False):
        C = self.C
        sq = c.hT
        last = c.accready
        lastact = None
        for (off, n) in c.segs:
            xb_op = self.vcopy(c.xbf[:, :, off:off + n], c.acc[:, :, off:off + n], deps=[last])
            sq_op = self.act(sq[:, :, off:off + n], c.acc[:, :, off:off + n], AF.Square, deps=[last])
            D0 = C.psD.next()
            D1 = C.psD.next()
            l0 = l1 = None
            for k in range(NCH):
                l0 = self.mm(D0.ap[:, 0:n], C.ones_b, c.xbf[:, k, off:off + n], k == 0, k == NCH - 1,
                             deps=[xb_op] + D0.readers if k == 0 else ())
            D0.readers = []
            for k in range(NCH):
                l1 = self.mm(D1.ap[:, 0:n], C.ones_b, sq[:, k, off:off + n], k == 0, k == NCH - 1,
                             deps=[sq_op] + D1.readers if k == 0 else ())
            D1.readers = []
            mean = C.mean[:, 0:n]
            rstd = C.rstd[:, 0:n]
            m_op = self.vts(mean, D0.ap[:, 0:n], 1.0 / D, None, ALU.mult, deps=[l0, xb_op])
            D0.readers.append(m_op)
            msq = self.vtt(C.t0[:, 0:n], mean, mean, ALU.mult, deps=[m_op])
            v_op = self.vstt(rstd, D1.ap[:, 0:n], 1.0 / D, C.t0[:, 0:n], ALU.mult, ALU.subtract,
                             deps=[l1, msq])
            D1.readers.append(v_op)
            r1 = self.vts(rstd, rstd, EPS, None, ALU.add, deps=[v_op])
            r2 = self.act(rstd, rstd, AF.Sqrt, deps=[r1])
            r_op = self.P.add("vector", lambda e, rstd=rstd: e.reciprocal(rstd, rstd), [r2])
            n_op = self.vstt(mean, mean, -1.0, rstd, ALU.mult, ALU.mult, deps=[r_op])
            prev = n_op
            o2s = []
            for k in range(NCH):
                a = c.acc[:, k, off:off + n]
                o1 = self.vtt(a, a, rstd, ALU.mult, deps=[n_op])
                o2s.append(self.vtt(a, a, mean, ALU.add, deps=[o1]))
            for k in range(NCH):
                a = c.acc[:, k, off:off + n]
                gi = idx * NCH + k
                if final:
                    prev = self.vts(a, a, C.lng[:, gi:gi + 1], C.lnb[:, gi:gi + 1], ALU.mult, ALU.add,
                                    deps=[o2s[k]])
                else:
                    o3 = self.act(c.xbf[:, k, off:off + n], a, AF.Identity,
                                  bias=C.lnb[:, gi:gi + 1], scale=C.lng[:, gi:gi + 1], deps=[o2s[k], l1])
                    prev = self.vts(a, a, C.lnga[:, gi:gi + 1], C.lnba[:, gi:gi + 1], ALU.mult, ALU.add,
                                    deps=[o3])
                    lastact = o3
            last = prev
        c.accready = last
        c.xready = [lastact, last]

    def load_tokens(self, c, pieces):
        C = self.C
        last_v = last_a = None
        for i, (rows, coff, n) in enumerate(pieces):
            sb = self.tokstg[i % 2]
            ld = self.P.dma("sync", sb.ap[0:n, :], rows, deps=list(sb.readers), dsem=sb.dsem)
            sb.readers = []
            for g in range(4):
                bk = C.psD.next()
                lt = None
                for j in range(4):
                    k = 4 * g + j
                    lt = self.tr(bk.ap[:, j * 128:j * 128 + n], sb.ap[0:n, k * 128:(k + 1) * 128],
                                 C.ident[0:n, 0:n], deps=[ld] + bk.readers if j == 0 else ())
                bk.readers = []
                src = bk.ap.rearrange("p (j t) -> p j t", j=4)[:, :, 0:n]
                last_a = self.acopy(c.acc[:, 4 * g:4 * g + 4, coff:coff + n], src, deps=[lt], scale=ALPHA)
                last_v = self.vcopy(c.xbf[:, 4 * g:4 * g + 4, coff:coff + n], src, deps=[lt, last_a])
                bk.readers += [last_a, last_v]
                sb.readers.append(lt)
        c.xready = [last_v]
        c.accready = [last_a, last_v]

    def store_tokens(self, c, pieces, scale=None):
        C = self.C
        last = None
        for i, (rows, coff, n) in enumerate(pieces):
            sb = self.tokstg[i % 2]
            evs = []
            for g in range(4):
                bk = C.psD.next()
                lt = None
                for j in range(4):
                    k = 4 * g + j
                    lt = self.tr(bk.ap[0:n, j * 128:(j + 1) * 128], c.acc[:, k, coff:coff + n],
                                 C.ident, deps=[c.accready] + bk.readers + sb.readers if j == 0 else ())
                bk.readers = []
                if g % 2 == 0:
                    ev = self.acopy(sb.ap[0:n, g * 512:(g + 1) * 512], bk.ap[0:n, :], deps=[lt], scale=scale)
                elif scale is None:
                    ev = self.vcopy(sb.ap[0:n, g * 512:(g + 1) * 512], bk.ap[0:n, :], deps=[lt])
                else:
                    ev = self.vts(sb.ap[0:n, g * 512:(g + 1) * 512], bk.ap[0:n, :], scale, None, ALU.mult,
                                  deps=[lt])
                bk.readers.append(ev)
                evs.append(ev)
            sb.readers = []
            if rows is not None:
                last = self.P.dma("sync", rows, sb.ap[0:n, :], deps=evs, dsem=sb.dsem)
                sb.readers.append(last)
                self.out_dmas.append(last)
            else:
                last = evs
        return last

    def pool_prompt(self, c, halo_in, halo_out, first, slot=0):
        C = self.C
        T = c.T
        L = T + 16
        ext = self.f32view(0, L)
        A = self.f32view(2 * L, L)
        Bb = self.f32view(4 * L, L)
        dbuf = c.hT
        hs = self.acopy(halo_out, c.acc[:, :, T - 16:T], deps=[c.accready])
        z1 = self.P.add("vector", lambda e: e.memset(A, 0.0), [c.accready, c.xready])
        z2 = self.P.add("vector", lambda e: e.memset(Bb, 0.0), [c.accready, c.xready])
        prev = [c.accready, c.xready, z1, z2]
        dop = None
        for k in range(NCH):
            g = k // 4
            w = 2 ** (g + 1)
            if first:
                e0 = self.vts(ext[:, 0:16], halo_in[:, k, :], C.meta[:, slot:slot + 1], None, ALU.mult,
                              deps=[prev, self.halo_ready])
            else:
                e0 = self.vcopy(ext[:, 0:16], halo_in[:, k, :], deps=[prev, self.halo_ready])
            e1 = self.acopy(ext[:, 16:L], c.acc[:, k, :], deps=[prev])
            src, dst = ext, A
            sh = 1
            dd = [e0, e1]
            lastop = None
            for s_ in range(g + 1):
                lastop = self.vtt(dst[:, sh:L], src[:, sh:L], src[:, 0:L - sh], ALU.add, deps=dd)
                dd = [lastop]
                src = dst
                dst = Bb if dst is A else A
                sh *= 2
            dop = self.vstt(dbuf[:, k, :], src[:, 16:L], 1.0 / w, ext[:, 16:L], ALU.mult, ALU.subtract,
                            deps=[lastop])
            if first:
                cw = self.vts(C.t1[:, 0:16], C.corrw[:, 16 * g:16 * g + 16], C.meta[:, 16 + slot:17 + slot], 1.0 / w,
                              ALU.mult, ALU.add, deps=[dop])
                t_ = self.vtt(C.t0[:, 0:16], src[:, 16:32], C.t1[:, 0:16], ALU.mult, deps=[dop, cw])
                dop = self.vtt(dbuf[:, k, 0:16], C.t0[:, 0:16], ext[:, 16:32], ALU.subtract, deps=[t_])
            prev = [dop]
        self.pool_mm(c, dbuf, dop, hs)
        return hs

    def pool_mm(self, c, dbuf, d_ready, extra):
        C, ws = self.C, self.ws
        last_acc = [d_ready, c.accready]
        mpt = WC // 128
        for g in range(4):
            for wt in range(512 // WC):
                v, dl, sl = ws.get(self.wsrc("pw", g * 512, 512, wt * WC, WC), 4, WC, "pw")
                lastm = None
                for oc in range(mpt):
                    k_out = 4 * g + wt * mpt + oc
                    for (off, n) in c.segs:
                        Cb = C.psC.next()
                        for kc in range(4):
                            lastm = self.mm(Cb.ap[:, 0:n], None if v is None else v[:, kc, oc * 128:(oc + 1) * 128],
                                            dbuf[:, 4 * g + kc, off:off + n], kc == 0, kc == 3,
                                            deps=[dl, d_ready] + Cb.readers if kc == 0 else ())
                        Cb.readers = []
                        last_acc = self.vstt(c.acc[:, k_out, off:off + n], Cb.ap[:, 0:n],
                                             C.psc[:, k_out:k_out + 1], c.acc[:, k_out, off:off + n],
                                             ALU.mult, ALU.add, deps=[lastm, last_acc, extra])
                        Cb.readers.append(last_acc)
                ws.release(sl, lastm)
        c.accready = last_acc

    def pool_sample(self, c):
        C, P = self.C, self.P
        o = c.end
        prevS = self.f32view(o, NCH * 60).rearrange("p (c t) -> p c t", c=NCH)
        o += 2 * NCH * 60
        R = NCH * NSEQ
        ext = self.f32view(o, R * 32).rearrange("p (r t) -> p r t", r=R)
        o += 2 * R * 32
        A = self.f32view(o, R * 32).rearrange("p (r t) -> p r t", r=R)
        o += 2 * R * 32
        Bb = self.f32view(o, R * 32).rearrange("p (r t) -> p r t", r=R)
        o += 2 * R * 32
        stg = self.tokstg[1]
        ld = P.dma("sync", stg.ap[0:60, :], self.I.sp, deps=list(stg.readers), dsem=stg.dsem)
        stg.readers = []
        evs = []
        for g in range(4):
            bk = C.psD.next()
            lt = None
            for j in range(4):
                k = 4 * g + j
                lt = self.tr(bk.ap[:, j * 64:j * 64 + 60], stg.ap[0:60, k * 128:(k + 1) * 128],
                             C.ident[0:60, 0:60], deps=[ld] + bk.readers if j == 0 else ())
            bk.readers = []
            src = bk.ap[:, 0:256].rearrange("p (j t) -> p j t", j=4)[:, :, 0:60]
            ev = self.acopy(prevS[:, 4 * g:4 * g + 4, :], src, deps=[lt], scale=ALPHA)
            bk.readers.append(ev)
            evs.append(ev)
            stg.readers.append(lt)
        ext4 = ext.rearrange("p (c s) t -> p c s t", c=NCH)
        lastop = None
        e_ops = []
        for s in range(NSEQ):
            e_ops.append(self.vcopy(ext4[:, :, s, 1:16], prevS[:, :, s * 15:(s + 1) * 15], deps=[evs[-1]]))
            e_ops.append(self.vcopy(ext4[:, :, s, 16:32], c.acc[:, :, s * 16:(s + 1) * 16], deps=[c.accready]))
            e_ops.append(self.vts(ext4[:, :, s, 0:1], c.acc[:, :, s * 16:s * 16 + 1], 0.0, None, ALU.mult,
                                  deps=[c.accready]))
        z1 = self.P.add("vector", lambda e: e.memset(A, 0.0), [c.accready])
        z2 = self.P.add("vector", lambda e: e.memset(Bb, 0.0), [c.accready])
        src, dst = ext, A
        sh = 1
        lastop = [e_ops[-1], z1, z2]
        res = {}
        for s_ in range(4):
            r0 = 16 * s_
            lastop = self.vtt(dst[:, r0:R, sh:32], src[:, r0:R, sh:32], src[:, r0:R, 0:32 - sh], ALU.add,
                              deps=[lastop])
            res[s_] = dst
            src = dst
            dst = Bb if dst is A else A
            sh *= 2
        dS = c.hT
        dop = lastop
        for g in range(4):
            w = 2 ** (g + 1)
            r4 = res[g].rearrange("p (c s) t -> p c s t", c=NCH)
            for s in range(NSEQ):
                dop = self.vstt(dS[:, 4 * g:4 * g + 4, s * 16:(s + 1) * 16], r4[:, 4 * g:4 * g + 4, s, 16:32],
                                1.0 / w, ext4[:, 4 * g:4 * g + 4, s, 16:32], ALU.mult, ALU.subtract, deps=[dop])
        return dop

    def store_pool_sample(self, c):
        evs = self.store_tokens(c, [(None, 0, 64)], scale=1.0 / ALPHA)
        sb = self.tokstg[0]
        for s in range(NSEQ):
            d = self.P.dma("sync", self.O.ps[s * 15:(s + 1) * 15, :], sb.ap[s * 16 + 1:s * 16 + 16, :],
                           deps=evs, dsem=sb.dsem)
            sb.readers.append(d)
            self.out_dmas.append(d)

    def store_pool_prompt(self, halo):
        C = self.C
        for g in range(4):
            sb = C.stgr.next()
            bk = C.psD.next()
            lt = None
            for j in range(4):
                k = 4 * g + j
                lt = self.tr(bk.ap[0:16, j * 128:(j + 1) * 128], halo[:, k, :], C.ident,
                             deps=[self.halo_ready] + bk.readers + sb.readers if j == 0 else ())
            bk.readers = []
            ev = self.acopy(sb.ap[0:16, :], bk.ap[0:16, :], deps=[lt], scale=1.0 / ALPHA)
            bk.readers.append(ev)
            d = self.P.dma("sync", self.O.pp[:, g * 512:(g + 1) * 512], sb.ap[1:16, :], deps=[ev], dsem=sb.dsem)
            sb.readers = [d]
            self.out_dmas.append(d)

    def proj_prompt(self, c, ti, slot, own):
        C, ws, P = self.C, self.ws, self.P
        O, S = self.O, self.S
        t0 = ti * TP
        kv0 = slot * PCH + t0
        nb = TP // 128
        hpt = WC // 128
        ntile = D // WC
        evi = [0]
        wd = "win"

        def evac(dst, src, deps):
            evi[0] += 1
            if evi[0] % 2:
                return self.acopy(dst, src, deps=deps)
            return self.vcopy(dst, src, deps=deps)

        def fm(v, dl, j, dst):
            lastm = None
            for hi in range(hpt):
                for (off, n) in c.segs:
                    bk = C.psA.next()
                    for kc in range(16):
                        lastm = self.mm(bk.ap[:, 0:n], None if v is None else v[:, kc, hi * 128:(hi + 1) * 128],
                                        c.xbf[:, kc, off:off + n], kc == 0, kc == 15,
                                        deps=[dl, c.xready] + bk.readers if kc == 0 else ())
                    bk.readers = []
                    sb = C.stgbr.next()
                    ev = evac(sb.ap[:, 0:n], bk.ap[:, 0:n], [lastm] + sb.readers)
                    bk.readers.append(ev)
                    r0 = (hpt * j + hi) * 128
                    cbase = kv0 if dst is S.KV else t0
                    d = P.dma("sync", dst[r0:r0 + 128, cbase + off:cbase + off + n], sb.ap[:, 0:n],
                              deps=[ev], dsem=sb.dsem)
                    sb.readers = [d]
                    self.spill_dmas.append(d)
            return lastm

        def tm(v, dl, j, out_ap, vscratch):
            lastm = None
            for b in range(nb):
                bk = C.psB.next()
                for kc in range(16):
                    lastm = self.mm(bk.ap[:, 0:WC], c.xbf[:, kc, b * 128:(b + 1) * 128],
                                    None if v is None else v[:, kc, :], kc == 0, kc == 15,
                                    deps=[dl, c.xready] + bk.readers if kc == 0 else ())
                bk.readers = []
                ev = None
                if out_ap is not None:
                    sb = C.stgr.next()
                    ev = evac(sb.ap[:, 0:WC], bk.ap[:, 0:WC], [lastm] + sb.readers)
                    bk.readers.append(ev)
                    d = P.dma("sync", out_ap[t0 + b * 128:t0 + (b + 1) * 128, j * WC:(j + 1) * WC], sb.ap[:, 0:WC],
                              deps=[ev], dsem=sb.dsem)
                    sb.readers = [d]
                    self.out_dmas.append(d)
                if vscratch:
                    sb2 = C.stgbr.next()
                    ev2 = evac(sb2.ap[:, 0:WC], bk.ap[:, 0:WC], [lastm, ev] + sb2.readers)
                    bk.readers.append(ev2)
                    blk = (kv0 // 128) + b
                    dst = S.KV[D + hpt * j * 128:D + hpt * (j + 1) * 128, blk * 128:(blk + 1) * 128] \
                        .rearrange("(h p) d -> p h d", p=128)
                    d2 = P.dma("sync", dst, sb2.ap[:, 0:WC].rearrange("p (h d) -> p h d", h=hpt), deps=[ev2],
                               dsem=sb2.dsem)
                    sb2.readers = [d2]
                    self.spill_dmas.append(d2)
            return lastm

        if own:
            for j in range(ntile):
                v, dl, sl = ws.get(self.wsrc("win", 0, D, j * WC, WC), 16, WC, wd)
                lm = fm(v, dl, j, S.QT)
                ws.release(sl, lm)
        for j in range(ntile):
            v, dl, sl = ws.get(self.wsrc("win", 0, D, D + j * WC, WC), 16, WC, wd)
            lm = fm(v, dl, j, S.KV)
            if own:
                lm = tm(v, dl, j, O.kp, False)
            ws.release(sl, lm)
        for j in range(ntile):
            v, dl, sl = ws.get(self.wsrc("win", 0, D, 2 * D + j * WC, WC), 16, WC, wd)
            lm = tm(v, dl, j, O.vp if own else None, True)
            ws.release(sl, lm)
        v, dl, sl = ws.get(self.wsrc("win", 0, D, 3 * D, H), 16, H, wd)
        bk = C.psC.next()
        lastm = None
        for b in range(nb):
            for kc in range(16):
                lastm = self.mm(bk.ap[:, b * 16:(b + 1) * 16], c.xbf[:, kc, b * 128:(b + 1) * 128],
                                None if v is None else v[:, kc, :], kc == 0, kc == 15,
                                deps=[dl, c.xready] + bk.readers + self.lf_readers if (kc == 0 and b == 0) else ())
        bk.readers = []
        ws.release(sl, lastm)
        lf = C.LF[:, ti * nb:(ti + 1) * nb, :]
        lf2 = C.LF.rearrange("p b h -> p (b h)")[:, ti * nb * 16:(ti + 1) * nb * 16]
        lop = self.logsig(lf2, bk.ap[:, 0:nb * 16], C.bf8[:, 0:nb * 16], nb * 16, 128, [lastm] + self.lf_readers, bk)
        self.lf_ready = lop
        if own:
            d = P.dma("sync", O.lp[t0:t0 + TP, :].rearrange("(b p) h -> p b h", p=128), lf, deps=[lop],
                      dsem=self.ds("lfout"))
            self.out_dmas.append(d)
            d = P.dma("sync", S.XR[:, t0:t0 + TP].rearrange("(c p) t -> p c t", p=128), c.acc,
                      deps=[c.accready], dsem=self.ds("xrspill"))
            self.spill_dmas.append(d)

    def logsig(self, out, zin, bias, n, npart, deps, bank=None):
        C = self.C
        z = C.t0[0:npart, 0:n]
        e = C.t1[0:npart, 0:n]
        o1 = self.vtt(z, zin, bias, ALU.add, deps=deps)
        if bank is not None:
            bank.readers.append(o1)
        o2a = self.vts(e, z, -1.0, None, ALU.mult, deps=[o1])
        o2 = self.vtt(e, e, z, ALU.max, deps=[o2a])
        o3 = self.act(e, e, AF.Exp, scale=-1.0, deps=[o2])
        o4 = self.act(e, e, AF.Ln, bias=1.0, deps=[o3])
        o5 = self.vts(z, z, 0.0, None, ALU.min, deps=[o4])
        o6 = self.vtt(out, z, e, ALU.subtract, deps=[o5])
        return o6

    def suffix_sums(self, slot):
        C, P = self.C, self.P
        bk = self.banks[6]
        bk2 = self.banks[7]
        lf = C.LF
        deps0 = [self.lf_ready] + bk.readers + bk2.readers
        last = None
        first = True
        for b in range(NBLK):
            n = NBLK - b
            for i, bb in enumerate(range(b, NBLK)):
                lhs = C.lstrict if bb == b else C.ones
                last = self.mm(bk.ap[:, b * 16:(b + 1) * 16], lhs, lf[:, bb, :], i == 0, i == n - 1,
                               deps=deps0 if first else ())
                first = False
        ev = self.vcopy(C.SUF.rearrange("p b h -> p (b h)"), bk.ap[:, 0:256], deps=[last] + self.lf_readers)
        for b in range(NBLK):
            last = self.mm(bk2.ap[:, 0:16], C.ones, lf[:, b, :], b == 0, b == NBLK - 1)
        for g in range(4):
            nb_ = 4 * g + 4
            for b in range(nb_):
                last = self.mm(bk2.ap[:, 16 + 16 * g:32 + 16 * g], C.ones, lf[:, b, :], b == 0, b == nb_ - 1)
        ev2 = self.vcopy(C.TT, bk2.ap[:, 0:16], deps=[last] + self.lf_readers)
        ev3 = self.vcopy(C.PRE.rearrange("p g h -> p (g h)"), bk2.ap[:, 16:80], deps=[last])
        bk.readers = [ev]
        bk2.readers = [ev2, ev3]
        r0 = slot * SFROWS
        d1 = P.dma("sync", self.S.SFG[r0:r0 + PCH, :].rearrange("(b p) h -> p b h", p=128), C.SUF, deps=[ev],
                   dsem=self.ds("sfspill"))
        d2 = P.dma("sync", self.S.SFG[r0 + PCH:r0 + PCH + 128, :], C.TT, deps=[ev2], dsem=self.ds("sfspill"))
        self.spill_dmas += [d1, d2]
        self.suf_ready = [ev, ev2, ev3]
        self.lf_readers = [last, d1, d2]

    def load_phase2(self, c, t0):
        P, S = self.P, self.S
        d1 = P.dma("sync", c.acc, S.XR[:, t0:t0 + TP].rearrange("(c p) t -> p c t", p=128),
                   deps=list(self.spill_dmas), dsem=self.ds("p2acc"))
        c.accready = d1
        if STAGE >= 3:
            d2 = P.dma("sync", c.xbf, S.OT[:, t0:t0 + TP].rearrange("(c p) t -> p c t", p=128),
                       deps=list(self.ot_dmas), dsem=self.ds("p2xbf"))
            c.xready = [d2]
        else:
            c.xready = None

    def wo(self, c):
        C, ws = self.C, self.ws
        last_acc = c.accready
        mpt = WC // 128
        for wt in range(D // WC):
            v, dl, sl = ws.get(self.wsrc("wo", 0, D, wt * WC, WC), 16, WC, "wo")
            lastm = None
            for ci in range(mpt):
                k_out = wt * mpt + ci
                for (off, n) in c.segs:
                    Cb = C.psC.next()
                    for kc in range(16):
                        lastm = self.mm(Cb.ap[:, 0:n], None if v is None else v[:, kc, ci * 128:(ci + 1) * 128],
                                        c.xbf[:, kc, off:off + n], kc == 0, kc == 15,
                                        deps=[dl, c.xready] + Cb.readers if kc == 0 else ())
                    Cb.readers = []
                    last_acc = self.vtt(c.acc[:, k_out, off:off + n], Cb.ap[:, 0:n], c.acc[:, k_out, off:off + n],
                                        ALU.add, deps=[lastm, last_acc])
                    Cb.readers.append(last_acc)
            ws.release(sl, lastm)
        c.accready = last_acc

    def proj_sample(self, c):
        C, ws, P = self.C, self.ws, self.P
        O = self.O
        o = c.end
        self.QTs = self.arena[:, o:o + H * 64].rearrange("p (h t) -> p h t", h=H)
        o += H * 64
        self.KTs = self.arena[:, o:o + H * 64].rearrange("p (h t) -> p h t", h=H)
        o += H * 64
        self.Vn = self.arena[0:16, o:o + NSEQ * D].rearrange("p (s d) -> p s d", s=NSEQ)
        o += NSEQ * D
        self.LFn = self.f32view(o, NSEQ * H)[0:16, :].rearrange("p (s h) -> p s h", s=NSEQ)
        o += 2 * NSEQ * H
        self.sa_base = o
        hpt = WC // 128
        wd = "win"
        lastev = None
        self.vn_ready = None
        for part, dstT in ((0, self.QTs), (1, self.KTs)):
            for j in range(D // WC):
                v, dl, sl = ws.get(self.wsrc("win", 0, D, part * D + j * WC, WC), 16, WC, wd)
                lastm = None
                for hi in range(hpt):
                    bk = C.psA.next()
                    for kc in range(16):
                        lastm = self.mm(bk.ap[:, 0:64], None if v is None else v[:, kc, hi * 128:(hi + 1) * 128],
                                        c.xbf[:, kc, 0:64], kc == 0, kc == 15,
                                        deps=[dl, c.xready] + bk.readers if kc == 0 else ())
                    bk.readers = []
                    ev = self.acopy(dstT[:, hpt * j + hi, :], bk.ap[:, 0:64], deps=[lastm])
                    bk.readers.append(ev)
                    lastev = ev
                if part == 1:
                    lastm = self.tm_sample(c, v, dl, j, O.ks, None)
                ws.release(sl, lastm)
        for j in range(D // WC):
            v, dl, sl = ws.get(self.wsrc("win", 0, D, 2 * D + j * WC, WC), 16, WC, wd)
            lastm = self.tm_sample(c, v, dl, j, O.vs, self.Vn)
            ws.release(sl, lastm)
        v, dl, sl = ws.get(self.wsrc("win", 0, D, 3 * D, H), 16, H, wd)
        bk = C.psC.next()
        lastm = None
        for s in range(NSEQ):
            for kc in range(16):
                lastm = self.mm(bk.ap[0:16, s * 16:(s + 1) * 16], c.xbf[:, kc, s * 16:(s + 1) * 16],
                                None if v is None else v[:, kc, :], kc == 0, kc == 15,
                                deps=[dl, c.xready] + bk.readers if (kc == 0 and s == 0) else ())
        bk.readers = []
        ws.release(sl, lastm)
        lop = self.logsig(self.LFn.rearrange("p s h -> p (s h)"), bk.ap[0:16, 0:64], C.bf8[0:16, 0:64], 64, 16,
                          [lastm], bk)
        d = P.dma("sync", O.ls.rearrange("(s p) h -> p s h", p=16), self.LFn, deps=[lop], dsem=self.ds("lsout"))
        self.out_dmas.append(d)
        self.sproj_ready = [lop, lastev, self.vn_ready]

    def tm_sample(self, c, v, dl, j, out_ap, vn):
        C, P = self.C, self.P
        lastm = None
        for s in range(NSEQ):
            bk = C.psB.next()
            for kc in range(16):
                lastm = self.mm(bk.ap[0:16, 0:WC], c.xbf[:, kc, s * 16:(s + 1) * 16],
                                None if v is None else v[:, kc, :], kc == 0, kc == 15,
                                deps=[dl, c.xready] + bk.readers if kc == 0 else ())
            bk.readers = []
            sb = C.stgr.next()
            ev = self.acopy(sb.ap[0:16, 0:WC], bk.ap[0:16, 0:WC], deps=[lastm] + sb.readers)
            bk.readers.append(ev)
            d = P.dma("sync", out_ap[s * 16:(s + 1) * 16, j * WC:(j + 1) * WC], sb.ap[0:16, 0:WC], deps=[ev],
                      dsem=sb.dsem)
            sb.readers = [d]
            self.out_dmas.append(d)
            if vn is not None:
                ev2 = self.vcopy(vn[:, s, j * WC:(j + 1) * WC], bk.ap[0:16, 0:WC], deps=[lastm, ev])
                bk.readers.append(ev2)
                self.vn_ready = ev2
        return lastm

    def attn_sample(self, c):
        C, P = self.C, self.P
        I = self.I
        o = self.sa_base
        kb = Buf(self.arena[:, o:o + 8192].rearrange("p (b n) -> p b n", b=16), self.ds("kc"))
        o += 8192
        vc_b = []
        for i in range(1):
            vc_b.append(Buf(self.arena[:, o:o + 8192].rearrange("p (b n) -> p b n", b=16), self.ds("vc%d" % i)))
            o += 8192
        ktc = []
        for i in range(2):
            ktc.append(Buf(self.arena[:, o:o + 4 * 2048].rearrange("p (h t) -> p h t", h=4)))
            o += 8192
        lfc = Buf(self.f32view(o, 256).rearrange("p (b h) -> p b h", b=16), self.ds("lfc"))
        o += 512
        Rb = self.f32view(o, 17 * 16).rearrange("p (b h) -> p b h", b=17)
        o += 2 * 17 * 16
        pts = [Buf(self.arena[:, o + 16 * i:o + 16 * i + 16]) for i in range(4)]
        o += 64
        rcp = self.f32view(o, 16)
        o += 32
        assert o <= TOKSTG, o
        oT = c.hT
        ptr = Rot(pts)
        gi = 0
        last_norm = None
        for s in range(NSEQ):
            ld = P.dma("sync", lfc.ap, I.cl[s].rearrange("(b p) h -> p b h", p=128),
                       deps=list(lfc.readers), dsem=lfc.dsem)
            lfc.readers = []
            bk = self.banks[6]
            last = None
            first = True
            for b in range(16):
                for i, bb in enumerate(range(b, 16)):
                    lhs = C.lstrict if bb == b else C.ones
                    last = self.mm(bk.ap[:, b * 16:(b + 1) * 16], lhs, lfc.ap[:, bb, :], i == 0, False,
                                   deps=[ld] + bk.readers + self.sproj_ready if first else ())
                    first = False
                last = self.mm(bk.ap[:, b * 16:(b + 1) * 16], C.ones[0:16, :], self.LFn[:, s, :], False, True)
            last = self.mm(bk.ap[0:16, 256:272], C.lstrict[0:16, 0:16], self.LFn[:, s, :], True, True)
            bk.readers = []
            lfc.readers.append(last)
            rdeps = [last, last_norm]
            ev = self.vcopy(Rb.rearrange("p b h -> p (b h)")[:, 0:256], bk.ap[:, 0:256], deps=rdeps)
            ev2 = self.vcopy(Rb[0:16, 16, :], bk.ap[0:16, 256:272], deps=rdeps)
            bk.readers += [ev, ev2]
            r_ready = [ev, ev2]
            for hg in range(4):
                vb = vc_b[0]
                kt = ktc[gi % 2]
                gi += 1
                dk = P.dma("gpsimd", kb.ap, I.ck[s, :, hg * 512:(hg + 1) * 512].rearrange("(b p) n -> p b n", p=128),
                           deps=list(kb.readers), dsem=kb.dsem)
                kb.readers = []
                dv = P.dma("gpsimd", vb.ap, I.cv[s, :, hg * 512:(hg + 1) * 512].rearrange("(b p) n -> p b n", p=128),
                           deps=list(vb.readers), dsem=vb.dsem)
                vb.readers = []
                kt_last = None
                for hi in range(4):
                    for q4 in range(4):
                        bk2 = C.psA.next()
                        bv = bk2.ap.bitcast(BF16)
                        lt = None
                        for j in range(4):
                            b = q4 * 4 + j
                            lt = self.tr(bv[:, j * 128:(j + 1) * 128], kb.ap[:, b, hi * 128:(hi + 1) * 128],
                                         C.ident_b, deps=[dk] + bk2.readers + kt.readers if j == 0 else ())
                        bk2.readers = []
                        evk = self.vcopy(kt.ap[:, hi, q4 * 512:(q4 + 1) * 512], bv[:, 0:512], deps=[lt])
                        bk2.readers.append(evk)
                        kt_last = evk
                        kb.readers.append(lt)
                kt.readers = []
                m3 = None
                for hi in range(4):
                    h = hg * 4 + hi
                    q = self.QTs[:, h, s * 16:(s + 1) * 16]
                    Ob = C.psC.next()
                    Sb = C.psD.next()
                    m1 = None
                    for b in range(17):
                        SB = C.psB.next()
                        if b < 16:
                            m1 = self.mm(SB.ap[:, 0:16], kt.ap[:, hi, b * 128:(b + 1) * 128], q, True, True,
                                         deps=[kt_last] + SB.readers + self.sproj_ready)
                            np_ = 128
                        else:
                            m1 = self.mm(SB.ap[0:16, 0:16], self.KTs[:, h, s * 16:(s + 1) * 16], q, True, True,
                                         deps=SB.readers + self.sproj_ready)
                            np_ = 16
                        SB.readers = []
                        pt = ptr.next()
                        e1 = self.act(pt.ap[0:np_, :], SB.ap[0:np_, 0:16], AF.Exp, bias=Rb[0:np_, b, h:h + 1],
                                      scale=SCALE, deps=[m1] + pt.readers + r_ready)
                        SB.readers.append(e1)
                        pt.readers = []
                        if b == 16:
                            e1 = self.vtt(pt.ap[0:16, :], pt.ap[0:16, :], C.tri_b[0:16, 0:16], ALU.mult, deps=[e1])
                            vv = self.Vn[:, s, h * 128:(h + 1) * 128]
                            on = C.ones_b[0:16, :]
                        else:
                            vv = vb.ap[:, b, hi * 128:(hi + 1) * 128]
                            on = C.ones_b
                        self.mm(Ob.ap[:, 0:16], vv, pt.ap[0:np_, :], b == 0, b == 16,
                                deps=[e1, dv] + (Ob.readers if b == 0 else []))
                        m3 = self.mm(Sb.ap[:, 0:16], on, pt.ap[0:np_, :], b == 0, b == 16,
                                     deps=(Sb.readers if b == 0 else []))
                        pt.readers.append(m3)
                    Ob.readers = []
                    Sb.readers = []
                    n1 = self.P.add("vector", lambda e, Sb=Sb: e.reciprocal(rcp, Sb.ap[:, 0:16]), [m3, last_norm])
                    n2 = self.vtt(oT[:, h, s * 16:(s + 1) * 16], Ob.ap[:, 0:16], rcp, ALU.mult, deps=[n1])
                    Ob.readers.append(n2)
                    Sb.readers.append(n1)
                    last_norm = n2
                    kt.readers.append(m1)
                vb.readers.append(m3)
        mv = self.vcopy(c.xbf[:, :, 0:64], oT[:, :, 0:64], deps=[last_norm])
        c.xready = [mv]

    def gather(self):
        self.kv_gathered = None

    def attn_prompt(self):
        C, P, S, I = self.C, self.P, self.S, self.I
        P.barrier()
        o = 0
        kvr = []
        for i in range(3):
            kvr.append(Buf(self.arena[:, o:o + 4096], self.ds("kvr%d" % i)))
            o += 4096
        qb = []
        for i in range(2):
            qb.append(Buf(self.arena[:, o:o + 2048], self.ds("qb%d" % i)))
            o += 2048
        pts = []
        for i in range(4):
            pts.append(Buf(self.arena[:, o:o + 512]))
            o += 512
        ost = []
        for i in range(2):
            ost.append(Buf(self.arena[:, o:o + 512], self.ds("ost%d" % i)))
            o += 512
        SUFA = self.f32view(o, 8 * 256).rearrange("p (r b h) -> p r b h", r=8, b=16)
        o += 2 * 8 * 256
        X = self.f32view(o, 8 * 64).rearrange("p (r g h) -> p r g h", r=8, g=4)
        o += 2 * 8 * 64
        Bh = []
        for i in range(2):
            Bh.append(Buf(self.f32view(o, 8 * 64).rearrange("p (r b g) -> p r b g", r=8, b=16)))
            o += 2 * 8 * 64
        rcp = self.f32view(o, 512)
        o += 1024
        ttm = self.f32view(o, 16)[0:8, :]
        o += 32
        wsl = self.f32view(o, 1024)[0:8, :]
        o += 2048
        assert o <= ARENA
        gd = list(self.spill_dmas)
        sd = self.ds("attsetup")
        lds = []
        for r in range(8):
            lds.append(P.dma("sync", SUFA[:, r], S.SFG[r * SFROWS:r * SFROWS + PCH, :]
                             .rearrange("(b p) h -> p b h", p=128), deps=[gd], dsem=sd))
        lds.append(P.dma("sync", ttm, S.SFG.rearrange("(r q) h -> r q h", r=8)[:, PCH, :], deps=[gd], dsem=sd))
        lds.append(P.dma("sync", wsl, I.wsel, dsem=sd))
        yb = self.banks[7]
        last = None
        for r in range(8):
            last = self.mm(yb.ap[:, r * 16:(r + 1) * 16], wsl[:, r * 128:(r + 1) * 128], ttm, True, True,
                           deps=[lds[-1]] + yb.readers if r == 0 else ())
        yb.readers = []
        xo = None
        for r in range(7):
            for g in range(4):
                xo = self.vstt(X[:, r, g, :], yb.ap[:, r * 16:(r + 1) * 16], C.meta[:, 8 + r:9 + r], C.PRE[:, g, :],
                               ALU.add, ALU.add, deps=[last, self.suf_ready])
        yb.readers.append(xo)
        for g in range(4):
            xo = self.vtt(X[:, 7, g, :], C.PRE[:, g, :], C.TT, ALU.subtract, deps=[self.suf_ready])
        x_ready = xo
        NSL = int(os.environ.get("MK_NSL", "8"))
        rlist = list(range(8 - NSL, 8))

        S0, S1 = self.banks[0], self.banks[1]
        S2, S3 = self.banks[2], self.banks[3]
        srot = Rot([S0, S1, S2, S3])
        prot = Rot(pts)
        accs = [(self.banks[4], self.banks[5]), (self.banks[6], self.banks[7])]
        kvi = 0
        self.ot_dmas = []
        ei = 0
        for h in range(H):
            q = qb[h % 2]
            qd = P.dma("sync", q.ap, S.QT[h * 128:(h + 1) * 128, :], deps=list(q.readers) + [self.spill_dmas],
                       dsem=q.dsem)
            q.readers = []
            bh = Bh[h % 2]
            bo = None
            for r in rlist:
                sfr = SUFA[:, r, :, h] if r < 7 else C.SUF[:, :, h]
                for g in range(4):
                    bo = self.vts(bh.ap[:, r, :, g], sfr, X[:, r, g, h:h + 1], None, ALU.add,
                                  deps=[x_ready, lds[0:8], bh.readers] if (r == rlist[0] and g == 0) else ())
            bh.readers = []
            for gp in range(2):
                first = [True, True]
                lastmm = [None, None]
                for r in rlist:
                    kv = kvr[kvi % 3]
                    kvi += 1
                    ksrc = S.KV[h * 128:(h + 1) * 128, r * PCH:(r + 1) * PCH]
                    vsrc = S.KV[D + h * 128:D + (h + 1) * 128, r * PCH:(r + 1) * PCH]
                    kdep = [gd]
                    kd = P.dma("sync", kv.ap[:, 0:2048], ksrc, deps=list(kv.readers) + kdep, dsem=kv.dsem)
                    vd = P.dma("sync", kv.ap[:, 2048:4096], vsrc, deps=[], dsem=kv.dsem)
                    kv.readers = []
                    KT = kv.ap[:, 0:2048]
                    V = kv.ap[:, 2048:4096].rearrange("p (b d) -> p b d", b=16)
                    lm = None
                    for blk in range(NBLK):
                        for gi_ in range(2):
                            G = 2 * gp + gi_
                            OTb, SMb = accs[gi_]
                            c0 = 0
                            diag = False
                            if r == 7:
                                j = blk - 4 * G
                                if j > 3:
                                    continue
                                if j >= 0:
                                    c0 = j * 128
                                    diag = True
                            islast = (r == 7 and blk == 4 * G + 3)
                            SB = srot.next()
                            m1 = self.mm(SB.ap[:, c0:512], KT[:, blk * 128:(blk + 1) * 128],
                                         q.ap[:, G * 512 + c0:(G + 1) * 512], True, True,
                                         deps=[kd, vd, qd] + SB.readers)
                            SB.readers = []
                            pt = prot.next()
                            e1 = self.act(pt.ap[:, c0:512], SB.ap[:, c0:512], AF.Exp, bias=bh.ap[:, r, blk, G:G + 1],
                                          scale=SCALE, deps=[m1, bo] + pt.readers)
                            SB.readers.append(e1)
                            pt.readers = []
                            if diag:
                                e1 = self.vtt(pt.ap[:, c0:c0 + 128], pt.ap[:, c0:c0 + 128], C.tri_b, ALU.mult,
                                              deps=[e1])
                            self.mm(OTb.ap[:, c0:512], V[:, blk, :], pt.ap[:, c0:512], first[gi_], islast,
                                    deps=[e1] + (OTb.readers if first[gi_] else []))
                            lm = self.mm(SMb.ap[:, c0:512], C.ones_b, pt.ap[:, c0:512], first[gi_], islast,
                                         deps=(SMb.readers if first[gi_] else []))
                            if first[gi_]:
                                OTb.readers = []
                                SMb.readers = []
                            first[gi_] = False
                            pt.readers.append(lm)
                            lastmm[gi_] = lm
                    kv.readers.append(lm)
                    bh.readers.append(lm)
                q.readers.append(lastmm[1])
                for gi_ in range(2):
                    G = 2 * gp + gi_
                    OTb, SMb = accs[gi_]
                    n1 = self.P.add("vector", lambda e, SMb=SMb: e.reciprocal(rcp, SMb.ap), [lastmm[gi_]])
                    ob = ost[ei % 2]
                    ei += 1
                    n2 = self.vtt(ob.ap, OTb.ap, rcp, ALU.mult, deps=[n1] + ob.readers)
                    OTb.readers.append(n2)
                    SMb.readers.append(n1)
                    d = P.dma("sync", S.OT[h * 128:(h + 1) * 128, G * 512:(G + 1) * 512], ob.ap, deps=[n2],
                              dsem=ob.dsem)
                    ob.readers = [d]
                    self.ot_dmas.append(d)
        P.barrier()

    def emit(self):
        P = self.P
        self.out_dmas = []
        self.spill_dmas = []
        self.ot_dmas = []
        self.halo_ready = None
        self.setup()
        self.prep_weights()
        C = self.C
        I, O = self.I, self.O

        def fin():
            if not P.dry:
                P.barrier(("tensor", "vector", "scalar", "sync", "gpsimd"))
                P.add("sync", None, [d for d in self.out_dmas if d is not None])
        if STOP <= 1:
            return fin()

        self.lf_readers = []
        cs = self.tile_ctx(192, [(0, 192)])
        self.load_tokens(cs, [(I.xs[:, :], 0, 64), (I.xh[:, :], 64, 128)])
        if STOP <= 2:
            return fin()
        self.ffn(cs, 0)
        if STOP <= 3:
            return fin()
        self.ln(cs, 0)
        hsp = None
        for sl_ in range(8):
            hsp = P.dma("sync", self.S.HSC[:, sl_ * 256:(sl_ + 1) * 256].rearrange("p (c t) -> p c t", c=NCH),
                        cs.acc[:, :, 64 + 16 * sl_:80 + 16 * sl_], deps=[cs.accready], dsem=self.ds("hsc"))
        self.store_pool_sample(cs)
        if STOP <= 4:
            return fin()
        dS_ready = self.pool_sample(cs)
        cs.segs = [(0, 64)]
        self.pool_mm(cs, cs.hT, dS_ready, None)
        self.ln(cs, 1)
        self.ffn(cs, 1)
        self.ln(cs, 2)
        self.ffn(cs, 2)
        self.ln(cs, 3)
        if STOP <= 5:
            return fin()
        self.proj_sample(cs)
        if STOP <= 6:
            return fin()
        if STAGE >= 2:
            self.attn_sample(cs)
            self.wo(cs)
        self.ln(cs, 4)
        self.ffn(cs, 3)
        self.ln(cs, 5, final=True)
        self.store_tokens(cs, [(O.ys[:, :], 0, 64)])
        P.barrier()
        if STOP <= 7:
            return fin()

        NSL = int(os.environ.get("MK_NSL", "8"))
        for slot in range(8 - NSL, 8):
            own = (slot == 7)
            hin, hout = C.halo[0], C.halo[1]
            for ti in range(PCH // TP):
                c = self.tile_ctx(TP, [(0, 512), (512, 512)])
                t0 = ti * TP
                r0 = slot * PCH + t0
                self.load_tokens(c, [(I.xp[r0 + 128 * b:r0 + 128 * (b + 1), :], 128 * b, 128)
                                     for b in range(TP // 128)])
                self.ffn(c, 0)
                self.ln(c, 0)
                if ti == 0:
                    self.halo_ready = P.dma("sync", hin,
                                            self.S.HSC[:, slot * 256:(slot + 1) * 256]
                                            .rearrange("p (c t) -> p c t", c=NCH),
                                            deps=[hsp, self.halo_ready], dsem=self.ds("hld"))
                self.halo_ready = self.pool_prompt(c, hin, hout, first=(ti == 0), slot=slot)
                if own and ti == PCH // TP - 1:
                    self.store_pool_prompt(hout)
                hin, hout = hout, hin
                self.ln(c, 1)
                self.ffn(c, 1)
                self.ln(c, 2)
                self.ffn(c, 2)
                self.ln(c, 3)
                self.proj_prompt(c, ti, slot, own)
                P.barrier()
            self.suffix_sums(slot)
        P.barrier()
        if STAGE >= 3:
            self.gather()
            self.attn_prompt()
        for ti in range(NPT):
            c = self.tile_ctx(TP, [(0, 512), (512, 512)])
            t0 = ti * TP
            self.load_phase2(c, t0)
            if STAGE >= 3:
                self.wo(c)
            self.ln(c, 4)
            self.ffn(c, 3)
            self.ln(c, 5, final=True)
            self.store_tokens(c, [(O.yp[t0 + 128 * b:t0 + 128 * (b + 1), :], 128 * b, 128)
                                  for b in range(TP // 128)])
            P.barrier()
        if not P.dry:
            P.add("sync", None, [d for d in self.out_dmas if d is not None])


_NC_CACHE = {}


def _consts():
    c = np.zeros((128, 512), np.float32)
    i = np.arange(128)
    c[:, 0:128] = np.eye(128, dtype=np.float32)
    c[:, 128:256] = (i[:, None] <= i[None, :]).astype(np.float32)
    c[:, 256:384] = (i[:, None] > i[None, :]).astype(np.float32)
    c[:, 384:512] = 1.0
    return c


def kernel(x_prompt, x_sample, state_pool, cache_fox_k, cache_fox_v, cache_fox_logf,
           ln_g, ln_b, ffn_w1, ffn_w3, ffn_w2, pool_w, pool_scale, fox_w_in, fox_b_f, fox_w_o):
    f = np.float32
    xp_full = np.asarray(x_prompt, f)[0]
    xs_full = np.asarray(x_sample, f)
    sp_full = np.asarray(state_pool, f)[0]
    ck = np.asarray(cache_fox_k, f)[0].reshape(32, -1, D)
    cv = np.asarray(cache_fox_v, f)[0].reshape(32, -1, D)
    cl = np.asarray(cache_fox_logf, f)[0]
    lng = np.ascontiguousarray(np.asarray(ln_g, f).reshape(6, NCH, 128).transpose(2, 0, 1).reshape(128, 96))
    lnb = np.ascontiguousarray(np.asarray(ln_b, f).reshape(6, NCH, 128).transpose(2, 0, 1).reshape(128, 96))
    psc = np.ascontiguousarray(np.asarray(pool_scale, f)[0].reshape(NCH, 128).T)
    bfr = np.ascontiguousarray(np.broadcast_to(np.asarray(fox_b_f, f)[0][None, :], (128, H)))
    w1 = np.ascontiguousarray(np.asarray(ffn_w1, f).reshape(4, D, -1)[:, :, :DFFX])
    w3 = np.ascontiguousarray(np.asarray(ffn_w3, f).reshape(4, D, -1)[:, :, :DFFX])
    w2 = np.ascontiguousarray(np.asarray(ffn_w2, f).reshape(4, -1, D)[:, :DFFX, :])
    pw = np.asarray(pool_w, f)[0].reshape(D, 512)
    win = np.asarray(fox_w_in, f)[0]
    wo = np.asarray(fox_w_o, f)[0]
    cst = _consts()
    in_maps = []
    for r in range(NRUN):
        chunks = [(r + 1 + i) % NCORE for i in range(NCORE)]
        xp = np.concatenate([xp_full[c * PCH:(c + 1) * PCH] for c in chunks], axis=0)
        xh = np.zeros((128, D), f)
        meta = np.zeros((128, 128), f)
        for sl, c in enumerate(chunks):
            if c > 0:
                xh[16 * sl:16 * sl + 16] = xp_full[c * PCH - 16:c * PCH]
            meta[:, sl] = 1.0 if c > 0 else 0.0
            meta[:, 8 + sl] = 0.0 if c < r else NEG
            meta[:, 16 + sl] = 1.0 if c == 0 else 0.0
        for g in range(4):
            w = 2 ** (g + 1)
            for t in range(16):
                meta[:, 32 + 16 * g + t] = w / min(t + 1, w)
        wsel = np.zeros((8, 8, 128), f)
        for rr in range(7):
            for rp in range(7):
                if chunks[rr] < chunks[rp] < r:
                    wsel[rp, rr, :] = 1.0
        m = {
            "xp": xp, "xh": xh, "xs": np.ascontiguousarray(xs_full[4 * r:4 * r + 4].reshape(64, D)),
            "sp": np.ascontiguousarray(sp_full[4 * r:4 * r + 4].reshape(60, D)),
            "ck": np.ascontiguousarray(ck[4 * r:4 * r + 4, :CPAST]), "cv": np.ascontiguousarray(cv[4 * r:4 * r + 4, :CPAST]),
            "cl": np.ascontiguousarray(cl[4 * r:4 * r + 4, :CPAST]),
            "w1": w1, "w3": w3, "w2": w2, "pw": pw, "win": win, "wo": wo,
            "psc": psc, "bf": bfr, "lng": lng, "lnb": lnb, "cst": cst,
            "meta": meta, "wsel": np.ascontiguousarray(wsel.reshape(8, 1024)),
        }
        in_maps.append(m)
    if "nc" not in _NC_CACHE:
        _NC_CACHE["nc"] = build_program()
    nc = _NC_CACHE["nc"]
    res = run_bass_kernel_spmd(nc, in_maps, core_ids=list(range(NRUN)))
    R = list(res.results)
    while len(R) < NCORE:
        R.append({k: np.zeros_like(np.asarray(v)) for k, v in R[0].items()})
    cat = lambda k: np.concatenate([np.asarray(R[r][k], f) for r in range(NCORE)], axis=0)
    y_prompt = cat("yp").reshape(1, NCORE * PCH, D)
    y_sample = cat("ys").reshape(32, TSQ, D)
    pool_p = np.asarray(R[NCORE - 1]["pp"], f).reshape(1, 1, 15, D)
    pool_s = cat("ps").reshape(1, 32, 15, D)
    k_p = cat("kp").reshape(1, 1, NCORE * PCH, H, 128)
    v_p = cat("vp").reshape(1, 1, NCORE * PCH, H, 128)
    lf_p = cat("lp").reshape(1, 1, NCORE * PCH, H)
    k_s = cat("ks").reshape(1, 32, TSQ, H, 128)
    v_s = cat("vs").reshape(1, 32, TSQ, H, 128)
    lf_s = cat("ls").reshape(1, 32, TSQ, H)
    return (y_prompt, y_sample, pool_p, pool_s, k_p, v_p, lf_p, k_s, v_s, lf_s)
```
